# Optimizing a Trainium2 kernel written in Bass

```python
import math
import jax, jax.numpy as jnp
from jax import lax
import numpy as np

D_MODEL = 1024
BATCH = 16
SEQ = 4096
DEPTH = 4
DEC_BATCH = 32
DEC_SEQ = 16
PAST_LEN = 4096

CHUNK = 64
QBLOCK = 128
N_BRANCH = 4
BRANCH_W = D_MODEL // N_BRANCH
CONV_W = BRANCH_W
CONV_K = 3
DIFF_HEADS = 4
DIFF_DH = BRANCH_W // (2 * DIFF_HEADS)
DIFF_VD = 2 * DIFF_DH
DIFF_QK = DIFF_HEADS * 2 * DIFF_DH
DIFF_V = DIFF_HEADS * DIFF_VD
FOX_HEADS = 4
FOX_DH = BRANCH_W // FOX_HEADS
FOX_QKV = FOX_HEADS * FOX_DH
POOL_WINDOWS = (2, 4, 8, 16)
POOL_GROUPS = 4
POOL_W = BRANCH_W
POOL_GC = POOL_W // POOL_GROUPS
POOL_HIST = 15
D_FF = ((8 * D_MODEL + 3 * 256 - 1) // (3 * 256)) * 256

IN_SPLITS = (CONV_W, CONV_W, CONV_W, DIFF_QK, DIFF_QK, DIFF_V, FOX_QKV, FOX_QKV, FOX_QKV, FOX_HEADS, POOL_W)
N_IN = 3 * CONV_W + 2 * DIFF_QK + DIFF_V + 3 * FOX_QKV + FOX_HEADS + POOL_W
NEG_INF = -1e30

kernel_name = 'hybrid_streaming_encoder_step'


def _rmsnorm(x, g, eps=1e-6):
    xf = x.astype(jnp.float32)
    y = xf * lax.rsqrt(jnp.mean(xf * xf, axis=-1, keepdims=True) + eps)
    return (y * g.astype(jnp.float32)).astype(x.dtype)


def _split_cols(y):
    outs, o = [], 0
    for n in IN_SPLITS:
        outs.append(y[..., o:o + n])
        o += n
    return outs


def _alibi_slopes():
    return 2.0 ** (-8.0 * (jnp.arange(DIFF_HEADS, dtype=jnp.float32) + 1.0) / DIFF_HEADS)


def _short_conv(z_ext, w):
    L = z_ext.shape[1] - (CONV_K - 1)
    return sum(w[i] * z_ext[:, i:i + L] for i in range(CONV_K))


def _pool_mix(u_ext, start_pos, w_grp, scale):
    B, T, _ = u_ext.shape
    L = T - POOL_HIST
    cs = jnp.cumsum(u_ext.astype(jnp.float32), axis=1)
    cs = jnp.concatenate([jnp.zeros((B, 1, POOL_W), jnp.float32), cs], axis=1)
    pos = start_pos + jnp.arange(L)
    u = u_ext[:, POOL_HIST:]
    outs = []
    for g, w in enumerate(POOL_WINDOWS):
        lo, hi = g * POOL_GC, (g + 1) * POOL_GC
        wsum = cs[:, POOL_HIST + 1:, lo:hi] - cs[:, POOL_HIST + 1 - w:POOL_HIST + 1 - w + L, lo:hi]
        cnt = jnp.minimum(w, pos + 1).astype(jnp.float32)[None, :, None]
        outs.append(wsum / cnt)
    d = (jnp.concatenate(outs, axis=-1) - u.astype(jnp.float32)).astype(u.dtype)
    d = d.reshape(B, L, POOL_GROUPS, POOL_GC)
    y = jnp.einsum('blgc,gce->blge', d, w_grp).reshape(B, L, POOL_W)
    return y * scale


def _diff_attn_core(q, k, v, qpos, kpos, lam):
    s = jnp.einsum('bqhcd,bkhcd->bhcqk', q, k).astype(jnp.float32) * (DIFF_DH ** -0.5)
    dist = jnp.abs(qpos[:, None] - kpos[None, :]).astype(jnp.float32)
    allowed = (kpos[None, :] // CHUNK) <= (qpos[:, None] // CHUNK)
    bias = jnp.where(allowed, -_alibi_slopes()[:, None, None] * dist, NEG_INF)
    p = jax.nn.softmax(s + bias[None, :, None], axis=-1)
    a = p[:, :, 0] - lam * p[:, :, 1]
    return jnp.einsum('bhqk,bkhe->bqhe', a.astype(v.dtype), v)


def _diff_attn_prompt(q, k, v, lam):
    B, S = q.shape[:2]
    nb = S // QBLOCK
    qb = jnp.moveaxis(q.reshape(B, nb, QBLOCK, DIFF_HEADS, 2, DIFF_DH), 1, 0)
    kpos = jnp.arange(S)

    def blk(args):
        qi, j = args
        return _diff_attn_core(qi, k, v, j * QBLOCK + jnp.arange(QBLOCK), kpos, lam)

    o = lax.map(blk, (qb, jnp.arange(nb)))
    return jnp.moveaxis(o, 0, 1).reshape(B, S, DIFF_HEADS, DIFF_VD)


def _fox_core(q, k, v, fq, fk, qpos, kpos):
    s = jnp.einsum('bqhd,bkhd->bhqk', q, k).astype(jnp.float32) * (FOX_DH ** -0.5)
    decay = jnp.swapaxes(fq, 1, 2)[..., :, None] - jnp.swapaxes(fk, 1, 2)[..., None, :]
    causal = kpos[None, :] <= qpos[:, None]
    p = jax.nn.softmax(jnp.where(causal, s + decay, NEG_INF), axis=-1)
    return jnp.einsum('bhqk,bkhd->bqhd', p.astype(v.dtype), v)


def _fox_prompt(q, k, v, F):
    B, S = q.shape[:2]
    nb = S // QBLOCK
    qb = jnp.moveaxis(q.reshape(B, nb, QBLOCK, FOX_HEADS, FOX_DH), 1, 0)
    Fb = jnp.moveaxis(F.reshape(B, nb, QBLOCK, FOX_HEADS), 1, 0)
    kpos = jnp.arange(S)

    def blk(args):
        qi, fi, j = args
        return _fox_core(qi, k, v, fi, F, j * QBLOCK + jnp.arange(QBLOCK), kpos)

    o = lax.map(blk, (qb, Fb, jnp.arange(nb)))
    return jnp.moveaxis(o, 0, 1).reshape(B, S, FOX_HEADS, FOX_DH)


def _layer(x, l, start_pos, hist, W):
    (g_norm, w_in, b_forget, conv_w, lambda_qk, diff_subln, pool_w, pool_scale,
     w_branch, w_gate, b_gate, w_out, w_ffn_in, w_ffn_out) = W
    B, L, _ = x.shape
    h = _rmsnorm(x, g_norm[l, 0])
    a_x, a_b, a_c, dq, dk, dv, fq, fk, fv, ff, pu = _split_cols(h @ w_in[l])

    z = a_c * a_x
    z_hist = jnp.zeros((B, CONV_K - 1, CONV_W), z.dtype) if hist is None else hist[5]
    z_ext = jnp.concatenate([z_hist, z], axis=1)
    o_a = a_b * _short_conv(z_ext, conv_w[l])
    new_conv = z_ext[:, -(CONV_K - 1):]

    dq = dq.reshape(B, L, DIFF_HEADS, 2, DIFF_DH)
    dk = dk.reshape(B, L, DIFF_HEADS, 2, DIFF_DH)
    dv = dv.reshape(B, L, DIFF_HEADS, DIFF_VD)
    lam_init = 0.8 - 0.6 * math.exp(-0.3 * l)
    lqk = lambda_qk[l].astype(jnp.float32)
    lam = jnp.exp(jnp.sum(lqk[0] * lqk[1])) - jnp.exp(jnp.sum(lqk[2] * lqk[3])) + lam_init
    if hist is None:
        o = _diff_attn_prompt(dq, dk, dv, lam)
    else:
        P = hist[0].shape[1]
        kk = jnp.concatenate([hist[0], dk], axis=1)
        vv = jnp.concatenate([hist[1], dv], axis=1)
        o = _diff_attn_core(dq, kk, vv, P + jnp.arange(L), jnp.arange(P + L), lam)
    o_b = (_rmsnorm(o, diff_subln[l]) * (1.0 - lam_init)).reshape(B, L, BRANCH_W)

    fq = fq.reshape(B, L, FOX_HEADS, FOX_DH)
    fk = fk.reshape(B, L, FOX_HEADS, FOX_DH)
    fv = fv.reshape(B, L, FOX_HEADS, FOX_DH)
    lf = jax.nn.log_sigmoid(ff.astype(jnp.float32) + b_forget[l].astype(jnp.float32))
    if hist is None:
        F = jnp.cumsum(lf, axis=1)
        oc = _fox_prompt(fq, fk, fv, F)
    else:
        P = hist[2].shape[1]
        F_all = jnp.cumsum(jnp.concatenate([hist[4].astype(jnp.float32), lf], axis=1), axis=1)
        kk = jnp.concatenate([hist[2], fk], axis=1)
        vv = jnp.concatenate([hist[3], fv], axis=1)
        oc = _fox_core(fq, kk, vv, F_all[:, P:], F_all, P + jnp.arange(L), jnp.arange(P + L))
    o_c = oc.reshape(B, L, BRANCH_W)

    u_hist = jnp.zeros((B, POOL_HIST, POOL_W), pu.dtype) if hist is None else hist[6]
    u_ext = jnp.concatenate([u_hist, pu], axis=1)
    o_d = _pool_mix(u_ext, start_pos, pool_w[l], pool_scale[l])
    new_pool = u_ext[:, -POOL_HIST:]

    merged = None
    for i, o_i in enumerate((o_a, o_b, o_c, o_d)):
        gate = jax.nn.sigmoid((h @ w_gate[l, i] + b_gate[l, i]).astype(jnp.float32)).astype(h.dtype)
        term = gate * (o_i @ w_branch[l, i])
        merged = term if merged is None else merged + term
    x = x + _rmsnorm(merged @ w_out[l], g_norm[l, 1])

    h2 = _rmsnorm(x, g_norm[l, 2])
    gu = h2 @ w_ffn_in[l]
    f = (jax.nn.silu(gu[..., :D_FF]) * gu[..., D_FF:]) @ w_ffn_out[l]
    x = x + _rmsnorm(f, g_norm[l, 3])
    return x, (dk, dv, fk, fv, lf, new_conv, new_pool)


def setup_inputs(seed: int = 0) -> dict:
    key = jax.random.key(seed)
    ks = jax.random.split(key, 24)
    nrm = jax.random.normal
    f32 = jnp.float32
    return {
        'x_prompt': nrm(ks[0], (BATCH, SEQ, D_MODEL), f32),
        'x_sample': nrm(ks[1], (DEC_BATCH, DEC_SEQ, D_MODEL), f32),
        'cache_diff_k': nrm(ks[2], (DEPTH, DEC_BATCH, PAST_LEN, DIFF_HEADS, 2, DIFF_DH), f32),
        'cache_diff_v': nrm(ks[3], (DEPTH, DEC_BATCH, PAST_LEN, DIFF_HEADS, DIFF_VD), f32),
        'cache_fox_k': nrm(ks[4], (DEPTH, DEC_BATCH, PAST_LEN, FOX_HEADS, FOX_DH), f32),
        'cache_fox_v': nrm(ks[5], (DEPTH, DEC_BATCH, PAST_LEN, FOX_HEADS, FOX_DH), f32),
        'cache_fox_lf': jax.nn.log_sigmoid(3.0 + nrm(ks[6], (DEPTH, DEC_BATCH, PAST_LEN, FOX_HEADS), f32)),
        'state_conv': nrm(ks[7], (DEPTH, DEC_BATCH, CONV_K - 1, CONV_W), f32),
        'state_pool': nrm(ks[8], (DEPTH, DEC_BATCH, POOL_HIST, POOL_W), f32),
        'g_norm': 1.0 + 0.05 * nrm(ks[9], (DEPTH, 4, D_MODEL), f32),
        'w_in': nrm(ks[10], (DEPTH, D_MODEL, N_IN), f32) * D_MODEL ** -0.5,
        'b_forget': jnp.linspace(1.0, 4.0, FOX_HEADS, dtype=f32)[None] + 0.1 * nrm(ks[11], (DEPTH, FOX_HEADS), f32),
        'conv_w': nrm(ks[12], (DEPTH, CONV_K, CONV_W), f32) * CONV_K ** -0.5,
        'lambda_qk': 0.1 * nrm(ks[13], (DEPTH, 4, DIFF_DH), f32),
        'diff_subln': 1.0 + 0.05 * nrm(ks[14], (DEPTH, DIFF_VD), f32),
        'pool_w': nrm(ks[15], (DEPTH, POOL_GROUPS, POOL_GC, POOL_GC), f32) * POOL_GC ** -0.5,
        'pool_scale': 1.0 + 0.05 * nrm(ks[16], (DEPTH, POOL_W), f32),
        'w_branch': nrm(ks[17], (DEPTH, N_BRANCH, BRANCH_W, D_MODEL), f32) * BRANCH_W ** -0.5,
        'w_gate': nrm(ks[18], (DEPTH, N_BRANCH, D_MODEL, D_MODEL), f32) * D_MODEL ** -0.5,
        'b_gate': 0.01 * nrm(ks[19], (DEPTH, N_BRANCH, D_MODEL), f32),
        'w_out': nrm(ks[20], (DEPTH, D_MODEL, D_MODEL), f32) * D_MODEL ** -0.5,
        'w_ffn_in': nrm(ks[21], (DEPTH, D_MODEL, 2 * D_FF), f32) * D_MODEL ** -0.5,
        'w_ffn_out': nrm(ks[22], (DEPTH, D_FF, D_MODEL), f32) * D_FF ** -0.5,
    }


def reference(x_prompt, x_sample, cache_diff_k, cache_diff_v, cache_fox_k, cache_fox_v, cache_fox_lf,
              state_conv, state_pool, g_norm, w_in, b_forget, conv_w, lambda_qk, diff_subln, pool_w,
              pool_scale, w_branch, w_gate, b_gate, w_out, w_ffn_in, w_ffn_out):
    W = (g_norm, w_in, b_forget, conv_w, lambda_qk, diff_subln, pool_w, pool_scale,
         w_branch, w_gate, b_gate, w_out, w_ffn_in, w_ffn_out)
    past = cache_diff_k.shape[2]
    yp, ys = x_prompt, x_sample
    p_states, s_states = [], []
    for l in range(DEPTH):
        yp, st = _layer(yp, l, 0, None, W)
        p_states.append(st)
        hist = (cache_diff_k[l], cache_diff_v[l], cache_fox_k[l], cache_fox_v[l], cache_fox_lf[l],
                state_conv[l], state_pool[l])
        ys, st = _layer(ys, l, past, hist, W)
        s_states.append(st)
    p_diff_k, p_diff_v, p_fox_k, p_fox_v, p_fox_lf, p_conv, p_pool = [jnp.stack(t) for t in zip(*p_states)]
    s_diff_k, s_diff_v, s_fox_k, s_fox_v, s_fox_lf, s_conv, s_pool = [jnp.stack(t) for t in zip(*s_states)]
    return (yp, ys, p_diff_k, p_diff_v, p_fox_k, p_fox_v, p_fox_lf, p_conv, p_pool,
            s_diff_k, s_diff_v, s_fox_k, s_fox_v, s_fox_lf, s_conv, s_pool)
```

```python
import numpy as np
import concourse.bass as bass
import concourse.mybir as mybir
from concourse.bass_utils import run_bass_kernel_spmd

F32 = mybir.dt.float32
BF16 = mybir.dt.bfloat16
ALU = mybir.AluOpType
AF = mybir.ActivationFunctionType

D = 1024
NIN = 2564
DFF = 2816
NFC = 22
SLOPES = [2.0 ** (-8.0 * (h + 1) / 4) for h in range(4)]
NEG = -30000.0
POOLW = (2, 4, 8, 16)
C_AX, C_AB, C_AC, C_DQ, C_DK, C_DV, C_FQ, C_FK, C_FV, C_FF, C_PU = 0, 256, 512, 768, 1024, 1280, 1536, 1792, 2048, 2304, 2308
DCH = [(0, 96), (96, 96), (192, 64)]

ENGS = ('pe', 'act', 'dve', 'pool', 'sp')


class Trk:
    __slots__ = ('w', 'r', 'excl', 'wb')

    def __init__(self, excl=False):
        self.w = {}
        self.wb = {}
        self.r = {}
        self.excl = excl


class Prog:
    def __init__(self, ndma=20):
        self.q = {e: [] for e in ENGS}
        self.tick = {e: 0 for e in ENGS}
        self.seen = {e: {} for e in ENGS}
        self.ndma = ndma
        self.dcnt = {}
        self.dnext = {e: 0 for e in ENGS}
        self.samewait = True

    def _need(self, eng, waits, ev):
        if ev is None:
            return
        k, v = ev
        if k == eng and (eng == 'pe' or not self.samewait):
            return
        if self.seen[eng].get(k, 0) >= v:
            return
        if waits.get(k, 0) < v:
            waits[k] = v

    def _deps(self, eng, R, W, WA=()):
        waits = {}
        for b in R:
            for k, v in b.w.items():
                self._need(eng, waits, (k, v))
        for b in W:
            for k, v in b.w.items():
                self._need(eng, waits, (k, v))
            for k, v in b.r.items():
                self._need(eng, waits, (k, v))
        for b in WA:
            for k, v in b.wb.items():
                self._need(eng, waits, (k, v))
            for k, v in b.r.items():
                self._need(eng, waits, (k, v))
        for k, v in waits.items():
            self.seen[eng][k] = v
            self.q[eng].append(('wait', k, v))

    def _mark(self, ev, R, W, WA=()):
        k, v = ev
        for b in R:
            if b.r.get(k, 0) < v:
                b.r[k] = v
        for b in W:
            b.w = {k: v}
            b.wb = {k: v}
            b.r = {}
        for b in WA:
            if b.w.get(k, 0) < v:
                b.w[k] = v

    def op(self, eng, fn, R=(), W=(), tick=True, WA=()):
        if any(b.excl for b in R):
            W = list(W) + [b for b in R if b.excl and b not in W]
            R = [b for b in R if not b.excl]
        self._deps(eng, R, W, WA)
        if tick:
            self.tick[eng] += 1
            ev = (eng, self.tick[eng])
            self.q[eng].append(('op', fn, (eng, 1)))
        else:
            ev = (eng, self.tick[eng] + 1)
            self.q[eng].append(('op', fn, None))
        self._mark(ev, R, W, WA)

    def dma(self, qe, out, in_, R=(), W=(), WA=(), pool=None, **kw):
        pl = pool or qe
        i = self.dnext.get(pl, 0)
        self.dnext[pl] = (i + 1) % self.ndma
        key = ('d', pl, i)
        n = self.dcnt.get(key, 0)
        waits = {}
        if n > 0:
            self._need(qe, waits, (key, 16 * n))
        for k, v in waits.items():
            self.seen[qe][k] = v
            self.q[qe].append(('wait', k, v))
        self._deps(qe, R, W, WA)
        self.dcnt[key] = n + 1
        ev = (key, 16 * (n + 1))
        self.q[qe].append(('op', (lambda e: e.dma_start(out=out, in_=in_, **kw)), (key, 16)))
        self._mark(ev, R, W, WA)

    def barrier(self, final=False):
        evs = [(e, self.tick[e]) for e in ('pe', 'act', 'dve', 'pool') if self.tick[e] > 0]
        evs += [(k, 16 * n) for k, n in self.dcnt.items() if final or k[1] != 'cv']
        for eng in ENGS:
            waits = {}
            for ev in evs:
                if ev[0] == eng:
                    continue
                self._need(eng, waits, ev)
            for k, v in waits.items():
                self.seen[eng][k] = v
                self.q[eng].append(('wait', k, v))

    def mm(self, out, lhsT, rhs, start=True, stop=True, R=(), W=(), tick=None):
        if tick is None:
            tick = stop
        self.op('pe', lambda e: e.matmul(out, lhsT=lhsT, rhs=rhs, start=start, stop=stop), R, W, tick)

    def tr(self, out, in_, ident, R=(), W=(), tick=True):
        self.op('pe', lambda e: e.transpose(out, in_, ident), R, W, tick)

    def act(self, out, in_, func, R=(), W=(), **kw):
        self.op('act', lambda e: e.activation(out, in_, func, **kw), R, W)

    def tt(self, eng, out, in0, in1, op, R=(), W=()):
        self.op(eng, lambda e: e.tensor_tensor(out, in0, in1, op), R, W)

    def ts(self, eng, out, in0, s1, s2, op0, op1=None, R=(), W=()):
        if op1 is None:
            self.op(eng, lambda e: e.tensor_scalar(out, in0, s1, None, op0), R, W)
        else:
            self.op(eng, lambda e: e.tensor_scalar(out, in0, s1, s2, op0, op1), R, W)

    def stt(self, eng, out, in0, scalar, in1, op0, op1, R=(), W=()):
        self.op(eng, lambda e: e.scalar_tensor_tensor(out, in0, scalar, in1, op0, op1), R, W)

    def cp(self, eng, out, in_, R=(), W=(), WA=()):
        if eng == 'act':
            self.op(eng, lambda e: e.activation(out, in_, AF.Copy), R, W, True, WA)
        else:
            self.op(eng, lambda e: e.tensor_copy(out, in_), R, W, True, WA)

    def memset(self, eng, ap, val, W=()):
        self.op(eng, lambda e: e.memset(ap, val), (), W)

    def recip(self, out, in_, R=(), W=()):
        self.op('dve', lambda e: e.reciprocal(out, in_), R, W)


class Arena:
    def __init__(self, ap_f32, nbytes):
        self.ap = ap_f32
        self.n = nbytes
        self.off = 0
        self.marks = []

    def push(self):
        self.marks.append(self.off)

    def pop(self):
        self.off = self.marks.pop()

    def get(self, free_shape, dt, parts=128):
        esz = 4 if dt == F32 else 2
        n = int(np.prod(free_shape))
        nb = (n * esz + 63) // 64 * 64
        assert self.off + nb <= self.n, ("SBUF arena overflow", self.off, nb, self.n)
        a = self.ap[:, self.off // 4:(self.off + nb) // 4]
        self.off += nb
        if dt != F32:
            a = a.bitcast(dt)
        a = a[:, 0:n]
        if len(free_shape) > 1:
            names = ' '.join('a%d' % i for i in range(len(free_shape)))
            kw = {'a%d' % i: int(free_shape[i]) for i in range(1, len(free_shape))}
            a = a.rearrange('p (%s) -> p %s' % (names, names), **kw)
        return a


def make_consts(S, NS=0, PL=0):
    c = {}
    c['ident'] = np.eye(128, dtype=np.float32)
    kk = np.arange(128)[:, None]
    c['tri'] = (kk <= np.arange(128)[None, :]).astype(np.float32)
    c['ones'] = np.ones((128, 128), np.float32)
    c['blk64'] = ((kk // 64) == (np.arange(128)[None, :] // 64)).astype(np.float32) / 64.0
    rel = np.arange(896)[None, :] - 384
    allowed = (kk // 64) <= np.floor_divide(rel, 64)
    md = np.zeros((128, 4, 896), np.float32)
    for h in range(4):
        v = np.where(kk <= rel, 0.0, -2.0 * SLOPES[h] * (kk - rel))
        md[:, h, :] = np.where(allowed, v, NEG)
    c['mwd'] = md
    c['mwf'] = np.where(kk <= rel, 0.0, NEG).astype(np.float32)
    c['twide'] = (kk <= rel).astype(np.float32)
    NQT = S // 512
    dmin = -512 * (NQT - 1) - 256 - 128
    ndi = (256 - dmin) // 128 + 1
    al = np.zeros((128, 4, ndi), np.float32)
    for h in range(4):
        for di in range(ndi):
            al[:, h, di] = SLOPES[h] * (np.arange(128) + dmin + 128 * di)
    c['alibi'] = al
    c['_dmin'] = dmin
    sel = np.zeros((4, 4, 128), np.float32)
    for h in range(4):
        sel[h, h, :] = 1.0
    c['sel4'] = sel
    ic = np.zeros((128, 2, 16), np.float32)
    for ch in range(2):
        for p in range(128):
            w = POOLW[ch * 2 + p // 64]
            ic[p, ch, :] = 1.0 / np.minimum(w, np.arange(16) + 1)
    c['invcnt'] = ic
    if NS:
        idx = np.arange(128)
        blk, tt_ = idx // 32, idx % 32
        same = (blk[:, None] == blk[None, :])
        c['bdtri'] = (same & (tt_[:, None] <= tt_[None, :])).astype(np.float32)
        mnd = np.full((128, 4, NS, 16), NEG, np.float32)
        mnf = np.full((128, NS, 16), NEG, np.float32)
        for i in range(NS):
            for tk in range(16):
                k = 32 * i + tk
                tq = np.arange(16)
                for h in range(4):
                    mnd[k, h, i, :] = SLOPES[h] * (tq - np.abs(tq - tk))
                mnf[k, i, :] = np.where(tk <= tq, 0.0, NEG)
        c['mnd'] = mnd
        c['mnf'] = mnf
        KT = PL // 128
        kb = np.zeros((1, 4, KT, 16), np.float32)
        for h in range(4):
            for kt in range(KT):
                kb[0, h, kt, :] = SLOPES[h] * (128 * kt - PL)
        c['ktb'] = kb
        c['bdtrib'] = c['bdtri']
        c['onesrow'] = np.ones((1, 128), np.float32)
    return c


CONST_BF = ('ident', 'mwd', 'mwf', 'twide', 'sel4', 'mnd', 'mnf', 'ktb', 'onesrow', 'bdtrib')


def build(cfg):
    NL, NSEQ, S = cfg['NL'], cfg['NSEQ'], cfg['S']
    T = 512
    NT = S // T
    NKT = S // 128
    consts = make_consts(S, cfg.get('NS', 0), cfg.get('PL', 0))
    dmin = consts.pop('_dmin')
    nc = bass.Bass("TRN2", target_bir_lowering=False)

    def din(name, shape, dt=F32):
        return nc.dram_tensor(name, list(shape), dt, kind="ExternalInput").ap()

    def dout(name, shape, dt=F32):
        return nc.dram_tensor(name, list(shape), dt, kind="ExternalOutput").ap()

    def dint(name, shape, dt=F32):
        return nc.dram_tensor(name, list(shape), dt, kind="Internal").ap()

    NS, PL = cfg.get('NS', 0), cfg.get('PL', 0)
    KT = PL // 128
    I = {}
    I['xp'] = din('xp', [NSEQ, S, D])
    if NS:
        I['xs'] = din('xs', [128, D])
        for nm in ('cdk', 'cdv', 'cfk', 'cfv'):
            I[nm] = din(nm, [NL, NS, PL, 256])
        I['clf'] = din('clf', [NL, NS, PL, 4])
        I['sconv'] = din('sconv', [NL, NS, 2, 256])
        I['spool'] = din('spool', [NL, NS, 15, 256])
    for nm, shp in (('g_norm', [NL, 4, D]), ('w_in', [NL, D, NIN]), ('b_forget', [NL, 4]), ('conv_w', [NL, 3, 256]),
                    ('lambda_qk', [NL, 4, 32]), ('diff_subln', [NL, 64]), ('pool_w', [NL, 4, 64, 64]),
                    ('pool_scale', [NL, 256]), ('w_branch', [NL, 4, 256, D]), ('w_gate', [NL, 4, D, D]),
                    ('b_gate', [NL, 4, D]), ('w_out', [NL, D, D]), ('w_ffn_in', [NL, D, 2 * DFF]),
                    ('w_ffn_out', [NL, DFF, D])):
        I[nm] = din(nm, shp)
    CI = {k: din('c_' + k, list(v.shape)) for k, v in consts.items()}

    O = {}
    O['yp'] = dout('yp', [NSEQ, S, D])
    for nm in ('p_dk', 'p_dv', 'p_fk', 'p_fv'):
        O[nm] = dout(nm, [NL, NSEQ, S, 256])
    O['p_lf'] = dout('p_lf', [NL, NSEQ, S, 4])
    O['p_conv'] = dout('p_conv', [NL, NSEQ, 2, 256])
    O['p_pool'] = dout('p_pool', [NL, NSEQ, 15, 256])
    if NS:
        O['ys'] = dout('ys', [128, D])
        for nm in ('s_dk', 's_dv', 's_fk', 's_fv'):
            O[nm] = dout(nm, [NL, 128, 256])
        O['s_lf'] = dout('s_lf', [NL, 128, 4])
        O['s_conv'] = dout('s_conv', [NL, NS, 2, 256])
        O['s_pool'] = dout('s_pool', [NL, NS, 15, 256])

    Wb = {}
    Wb['win'] = dint('wb_in', [NL, 128, 8, NIN], BF16)
    Wb['wg'] = dint('wb_g', [NL, 4, 128, 8, D], BF16)
    Wb['wbr'] = dint('wb_br', [NL, 128, 4, 2, D], BF16)
    Wb['wo'] = dint('wb_o', [NL, 128, 8, D], BF16)
    Wb['wfi'] = dint('wb_fi', [NL, 6, 128, 8, 2, 512], BF16)
    Wb['wfo'] = dint('wb_fo', [NL, 3, 128, 8, D], BF16)
    dbg = dout if cfg.get('debug') else dint
    xres = dbg('xres', [NSEQ, S, D])
    hT_scr = dbg('hT_scr', [128, 8, S], BF16)
    qT_scr = dbg('qT_scr', [128, 5, S], BF16)
    kT_scr = dbg('kT_scr', [128, 5, S], BF16)
    v_scr = dbg('v_scr', [128, NKT, 2, 2, 3, 64], BF16)
    nF_scr = dbg('nF_scr', [128, NKT, 4])
    cq_scr = dbg('cq_scr', [4, S], BF16)
    obr_scr = dbg('obr_scr', [4, 128, 2, S], BF16)
    SS = {}
    if NS:
        SS = dict(xres=dint('xres_s', [128, D]), hT=dint('hTs_scr', [128, 8, 128], BF16), qT=dint('qTs_scr', [128, 5, 128], BF16),
                  kT=dint('kTs_scr', [128, 5, 128], BF16), v=dint('vs_scr', [128, 4, 192], BF16),
                  nF=dint('nFs_scr', [128, NS, KT + 1, 4]), cq=dint('cqs_scr', [4, 128], BF16),
                  obr=dbg('obrs_scr', [4, 128, 2, 128], BF16))

    P = Prog()
    NARENA = 206 * 1024

    with nc.sbuf_tensor("arena", [128, NARENA // 4], F32) as arena_t:
        psum_ts = []
        import contextlib
        with contextlib.ExitStack() as es:
            for i in range(8):
                psum_ts.append(es.enter_context(nc.psum_tensor("ps%d" % i, [128, 512], F32)))
            sems = {}
            for e in ('pe', 'act', 'dve', 'pool'):
                sems[e] = es.enter_context(nc.semaphore("tick_" + e))
            for qe in ('sp', 'pool', 'cv'):
                for i in range(P.ndma):
                    sems[('d', qe, i)] = es.enter_context(nc.semaphore("dma_%s_%d" % (qe, i)))

            A = Arena(arena_t[:], NARENA)
            PS = [t[:] for t in psum_ts]
            PST = [Trk(excl=True) for _ in range(8)]

            gen_program(nc, P, A, PS, PST, cfg, I, CI, O, Wb, consts, dmin,
                        dict(xres=xres, hT=hT_scr, qT=qT_scr, kT=kT_scr, v=v_scr, nF=nF_scr, cq=cq_scr, obr=obr_scr), SS)

            P.barrier(final=True)

            with nc.allow_non_contiguous_dma(reason="small strided parameter / state transfers"), nc.Block() as block:
                def replay(e, items):
                    for it in items:
                        if it[0] == 'wait':
                            e.wait_ge(sems[it[1]], it[2])
                        else:
                            ins = it[1](e)
                            if it[2] is not None:
                                ins.then_inc(sems[it[2][0]], it[2][1])

                @block.tensor
                def _(e):
                    replay(e, P.q['pe'])

                @block.scalar
                def _(e):
                    replay(e, P.q['act'])

                @block.vector
                def _(e):
                    replay(e, P.q['dve'])

                @block.gpsimd
                def _(e):
                    replay(e, P.q['pool'])

                @block.sync
                def _(e):
                    replay(e, P.q['sp'])
    cfg['_ninstr'] = {e: len(P.q[e]) for e in ENGS}
    return nc


class _Stop(Exception):
    pass


def lam_init(l):
    import math
    return 0.8 - 0.6 * math.exp(-0.3 * l)


def gen_program(nc, P, A, PS, PST, cfg, I, CI, O, Wb, consts, dmin, SCR, SS):
    NL, NSEQ, S = cfg['NL'], cfg['NSEQ'], cfg['S']
    T = 512
    NT = S // T
    NKT = S // 128

    xs_t = A.get([4, D], F32)
    hT = A.get([8, T], BF16)
    hpre = A.get([4, D], BF16)
    mT = hpre.rearrange("p b d -> p (b d)").rearrange("p (c t) -> p c t", c=8)
    ssq = A.get([8], F32)
    rstd = A.get([8], F32)
    sm_t = Trk()
    XS, HT, HP = Trk(), Trk(), Trk()
    stage_f = xs_t.rearrange("p b d -> p (b d)")
    stg_t = XS

    C = {}
    CT = Trk()

    def shaped(ap2d, fs):
        if len(fs) > 1:
            names = ' '.join('a%d' % i for i in range(len(fs)))
            kw = {'a%d' % i: int(fs[i]) for i in range(1, len(fs))}
            return ap2d.rearrange('p (%s) -> p %s' % (names, names), **kw)
        return ap2d
    for k, v in consts.items():
        fs = list(v.shape[1:])
        np_ = v.shape[0]
        n = int(np.prod(fs))
        if k in CONST_BF:
            dst = A.get(fs, BF16)
            st = shaped(stage_f[:, 0:n], fs)
            P.dma('sp', st[0:np_], CI[k], W=[stg_t])
            P.cp('dve', dst[0:np_], st[0:np_], R=[stg_t], W=[CT])
        else:
            dst = A.get(fs, F32)
            P.dma('sp', dst[0:np_], CI[k], W=[CT])
        C[k] = dst
    identb = C['ident']
    identf = A.get([128], F32)
    P.dma('sp', identf, CI['ident'], W=[CT])

    gcol = A.get([NL, 4, 8], F32)
    bgcol = A.get([NL, 4, 8], F32)
    for l in range(NL):
        for n_ in range(4):
            P.dma('sp', gcol[:, l, n_, :], I['g_norm'][l, n_].rearrange("(c p) -> p c", p=128), WA=[CT])
            P.dma('sp', bgcol[:, l, n_, :], I['b_gate'][l, n_].rearrange("(c p) -> p c", p=128), WA=[CT])
    cwcol = A.get([NL, 3, 2], F32)
    for l in range(NL):
        for k_ in range(3):
            P.dma('sp', cwcol[:, l, k_, :], I['conv_w'][l, k_].rearrange("(c p) -> p c", p=128), WA=[CT])
    pscol = A.get([NL, 2], F32)
    for l in range(NL):
        P.dma('sp', pscol[:, l, :], I['pool_scale'][l].rearrange("(c p) -> p c", p=128), WA=[CT])
    gsub = A.get([NL], F32)
    for hh in range(2):
        P.dma('sp', gsub[64 * hh:64 * hh + 64, :], I['diff_subln'].rearrange("l e -> e l"), WA=[CT])
    for l in range(NL):
        P.ts('dve', gsub[:, l:l + 1], gsub[:, l:l + 1], 1.0 - lam_init(l), None, ALU.mult, R=[CT], W=[CT])
    bfg = A.get([NL, 4], F32)
    P.dma('sp', bfg.rearrange("p l h -> p (l h)"), I['b_forget'].rearrange("l h -> (l h)").partition_broadcast(128), W=[CT])
    epsc = A.get([1], F32)
    P.memset('dve', epsc, 1e-6, W=[CT])
    onec = A.get([1], F32)
    P.memset('dve', onec, 1.0, W=[CT])
    lqk = A.get([NL, 4, 32], F32)
    P.dma('sp', lqk.rearrange("p l a d -> p (l a d)"), I['lambda_qk'].rearrange("l a d -> (l a d)").partition_broadcast(128), W=[CT])
    lprod = A.get([NL, 2, 32], F32)
    P.tt('dve', lprod, lqk[:, :, 0:4:2, :], lqk[:, :, 1:4:2, :], ALU.mult, R=[CT], W=[CT])
    lsum = A.get([NL, 2], F32)
    P.op('dve', lambda e: e.reduce_sum(lsum, lprod, mybir.AxisListType.X), R=[CT], W=[CT])
    lexp = A.get([NL, 2], F32)
    P.act(lexp, lsum, AF.Exp, R=[CT], W=[CT])
    neglam = A.get([NL], F32)
    P.tt('dve', neglam, lexp[:, :, 1], lexp[:, :, 0], ALU.subtract, R=[CT], W=[CT])
    for l in range(NL):
        P.ts('dve', neglam[:, l:l + 1], neglam[:, l:l + 1], -lam_init(l), None, ALU.add, R=[CT], W=[CT])
    poolW = A.get([NL, 2, 128], BF16)
    pw_st = stage_f[:, 0:NL * 256].rearrange("p (l c e) -> p l c e", l=NL, c=2)
    P.memset('dve', pw_st, 0.0, W=[stg_t])
    for l in range(NL):
        for g in range(4):
            ch, hf = g // 2, g % 2
            P.dma('sp', pw_st[64 * hf:64 * hf + 64, l, ch, 64 * hf:64 * hf + 64], I['pool_w'][l, g], WA=[stg_t])
    P.cp('dve', poolW, pw_st, R=[stg_t], W=[CT])
    gbc = A.get([2, D], F32)
    gbc_t = Trk()

    WT = [Trk() for _ in range(NL)]

    def convert_layer(l):
        kw = dict(WA=[WT[l]], pool='cv')
        P.dma('pool', Wb['win'][l], I['w_in'][l].rearrange("(c p) n -> p c n", p=128), **kw)
        for i in range(4):
            P.dma('pool', Wb['wg'][l, i], I['w_gate'][l, i].rearrange("(c p) n -> p c n", p=128), **kw)
            P.dma('pool', Wb['wbr'][l, :, i], I['w_branch'][l, i].rearrange("(c p) n -> p c n", p=128), **kw)
        P.dma('pool', Wb['wo'][l], I['w_out'][l].rearrange("(c p) n -> p c n", p=128), **kw)
        wfi = I['w_ffn_in'][l].rearrange("(c p) n -> p c n", p=128)
        for j in range(6):
            w = 512 if j < 5 else 256
            for gu in range(2):
                P.dma('pool', Wb['wfi'][l, j, :, :, gu, 0:w], wfi[:, :, gu * DFF + 512 * j: gu * DFF + 512 * j + w], **kw)
        wfo = I['w_ffn_out'][l].rearrange("(c p) n -> p c n", p=128)
        for j in range(3):
            n = 8 if j < 2 else 6
            P.dma('pool', Wb['wfo'][l, j, :, 0:n, :], wfo[:, 8 * j:8 * j + n, :], **kw)

    convert_layer(0)
    P.barrier()

    scr_t = {k: Trk() for k in SCR}
    xres_t = Trk()
    psrot = {'i': 0}

    def chk(n):
        if cfg.get('stop', 99) == n:
            raise _Stop()

    def bank(lst):
        i = lst[psrot['i'] % len(lst)]
        psrot['i'] += 1
        return PS[i], PST[i]

    def norm_to_hT(l, n, nb=4, bp=128):
        P.memset('dve', ssq[:bp, 0:nb], 0.0, W=[sm_t])
        for b in range(nb):
            P.act(hpre[:bp, b, :], xs_t[:bp, b, :], AF.Square, R=[XS], W=[HP, sm_t], accum_out=ssq[:bp, b:b + 1])
        P.act(rstd[:bp, 0:nb], ssq[:bp, 0:nb], AF.Sqrt, R=[sm_t, CT], W=[sm_t], bias=epsc[:bp], scale=1.0 / D)
        P.recip(rstd[:bp, 0:nb], rstd[:bp, 0:nb], R=[sm_t], W=[sm_t])
        for b in range(nb):
            P.act(hpre[:bp, b, :], xs_t[:bp, b, :], AF.Identity, R=[XS, sm_t], W=[HP], scale=rstd[:bp, b:b + 1])
        for c in range(8):
            ps, pt = bank([0, 1])
            psb = ps.bitcast(BF16)
            for b in range(nb):
                P.tr(psb[:, b * bp:(b + 1) * bp], hpre[:bp, b, c * 128:(c + 1) * 128], identb[:bp, :bp],
                     R=[HP, CT], W=[pt], tick=(b == nb - 1))
            P.ts('dve', hT[:, c, 0:nb * bp], psb[:, 0:nb * bp], gcol[:, l, n, c:c + 1], None, ALU.mult,
                 R=[pt, CT], W=[HT])

    def pass_a(l, s):
        A.push()
        win = A.get([8, NIN], BF16)
        WIN = Trk()
        P.dma('sp', win, Wb['win'][l], R=[WT[l]], W=[WIN])
        stage = A.get([3, 512], F32)
        STG = [Trk() for _ in range(3)]
        vst = A.get([16, 3, 64], BF16)
        vst6 = vst.rearrange("p (k b a) s e -> p k b a s e", k=4, b=2)
        VST = Trk()
        P.memset('pool', vst[:, :, 1, :], 1.0, W=[VST])
        kst = A.get([5, T], BF16)
        KST = Trk()
        qst = A.get([5, T], BF16)
        QST = Trk()
        zext = A.get([2, T + 2], F32)
        ZX = Trk()
        P.memset('pool', zext[:, :, 0:2], 0.0, W=[ZX])
        axs = A.get([T], F32)
        AXS = Trk()
        y1 = A.get([T], F32)
        y2 = A.get([T], F32)
        YT = Trk()
        uext = A.get([2, T + 15], F32)
        UX = Trk()
        P.memset('pool', uext[:, :, 0:15], 0.0, W=[UX])
        sw = A.get([4, T + 15], F32)
        SW = Trk()
        dTt = A.get([2, T], BF16)
        DT = Trk()
        tmp16 = A.get([16], F32)
        oast = A.get([2, T], BF16)
        OAS = Trk()
        odst = A.get([2, T], BF16)
        ODS = Trk()
        lf = A.get([4, 4], F32)
        lfb = A.get([4, 4], BF16)
        lft = A.get([4, 4], F32)
        nFt = A.get([4, 4], F32)
        tot = A.get([4], F32)
        LF = Trk()
        TOT = Trk()
        P.memset('dve', tot, 0.0, W=[TOT])
        cqs = A.get([T], BF16)
        CQS = Trk()
        sti = {'i': 0}

        for t in range(NT if cfg.get('stop', 99) == 99 else 1):
          try:
            t0 = t * T
            if t > 0:
                P.cp('pool', zext[:, :, 0:2], zext[:, :, T:T + 2], R=[ZX], W=[ZX])
                P.cp('pool', uext[:, :, 0:15], uext[:, :, T:T + 15], R=[UX], W=[UX])
            src = I['xp'][s, t0:t0 + T, :] if l == 0 else SCR['xres'][s, t0:t0 + T, :]
            P.dma('sp', xs_t, src.rearrange("(b p) d -> p b d", p=128), R=[xres_t], W=[XS])
            chk(0)
            norm_to_hT(l, 0)
            chk(1)
            P.dma('sp', SCR['hT'][:, :, t0:t0 + T], hT, R=[HT], WA=[scr_t['hT']])
            pl, plt = PS[6], PST[6]
            for b in range(4):
                for gi, (c0, okn, ovn) in enumerate(((C_DK, 'p_dk', 'p_dv'), (C_FK, 'p_fk', 'p_fv'))):
                    ps, pt = bank([2, 3, 4, 5])
                    for c in range(8):
                        P.mm(ps, hT[:, c, b * 128:(b + 1) * 128], win[:, c, c0:c0 + 512], start=(c == 0), stop=(c == 7),
                             R=[HT, WIN], W=[pt])
                    si = sti['i'] % 3
                    sti['i'] += 1
                    P.cp('act', stage[:, si, :], ps, R=[pt], W=[STG[si]])
                    P.cp('dve', vst6[:, b, gi, :, 0:3:2, :], ps[:, 256:512].rearrange("p (a s e) -> p a s e", a=2, s=2),
                         R=[pt], W=[VST])
                    P.dma('pool', O[okn][l, s, t0 + b * 128:t0 + (b + 1) * 128, :], stage[:, si, 0:256], R=[STG[si]])
                    P.dma('pool', O[ovn][l, s, t0 + b * 128:t0 + (b + 1) * 128, :], stage[:, si, 256:512], R=[STG[si]])
                for c in range(8):
                    P.mm(pl[:, b * 4:(b + 1) * 4], hT[:, c, b * 128:(b + 1) * 128], win[:, c, C_FF:C_FF + 4],
                         start=(c == 0), stop=(c == 7), R=[HT, WIN], W=[plt])
            P.dma('sp', SCR['v'][:, 4 * t:4 * t + 4].rearrange("p k b a s e -> p (k b a) s e"), vst, R=[VST], WA=[scr_t['v']])
            chk(2)
            P.tt('dve', lft, pl[:, 0:16].rearrange("p (b h) -> p b h", h=4),
                 bfg[:, l:l + 1, :].to_broadcast([128, 4, 4]), ALU.add, R=[plt, CT], W=[LF])
            P.act(lft, lft, AF.Exp, R=[LF], W=[LF], scale=-1.0)
            P.act(lft, lft, AF.Ln, R=[LF, CT], W=[LF], bias=onec, scale=1.0)
            P.ts('dve', lf, lft, -1.0, None, ALU.mult, R=[LF], W=[LF])
            P.dma('pool', O['p_lf'][l, s, t0:t0 + T, :].rearrange("(b p) h -> p b h", p=128), lf, R=[LF])
            chk(3)
            pf, pft = PS[7], PST[7]
            for b in range(4):
                P.mm(pf[:, b * 4:(b + 1) * 4], C['tri'], lf[:, b, :], start=True, stop=(b == 0), R=[LF, CT], W=[pft])
                for b2 in range(b):
                    P.mm(pf[:, b * 4:(b + 1) * 4], C['ones'], lf[:, b2, :], start=False, stop=(b2 == b - 1), R=[LF, CT], W=[pft])
            for b in range(4):
                P.mm(pf[:, 16:20], C['ones'], lf[:, b, :], start=(b == 0), stop=(b == 3), R=[LF, CT], W=[pft])
            P.stt('dve', nFt, pf[:, 0:16].rearrange("p (b h) -> p b h", h=4), -1.0,
                  tot.unsqueeze(1).to_broadcast([128, 4, 4]), ALU.mult, ALU.subtract, R=[pft, TOT], W=[LF])
            P.dma('sp', SCR['nF'][:, 4 * t:4 * t + 4, :], nFt, R=[LF], WA=[scr_t['nF']])
            P.cp('dve', lfb, lf, R=[LF], W=[LF])
            P.tt('dve', lfb[0:1, 0, :], lf[0:1, 0, :], tot[0:1, :], ALU.add, R=[LF, TOT], W=[LF])
            P.tt('dve', tot, tot, pf[:, 16:20], ALU.add, R=[pft, TOT], W=[TOT])
            pc, pct = bank([2, 3, 4, 5])
            for b in range(4):
                P.mm(pc[0:4, :], lfb[:, b, :], C['twide'][:, 384 - 128 * b:384 - 128 * b + T], start=(b == 0), stop=(b == 3),
                     R=[LF, CT], W=[pct])
            P.cp('act', cqs[0:4, :], pc[0:4, :], R=[pct], W=[CQS])
            P.dma('sp', SCR['cq'][:, t0:t0 + T], cqs[0:4, :], R=[CQS], WA=[scr_t['cq']])

            chk(4)
            def fm(col0, m):
                ps, pt = bank([2, 3, 4, 5])
                for c in range(8):
                    P.mm(ps[0:m, :], win[:, c, col0:col0 + m], hT[:, c, :], start=(c == 0), stop=(c == 7), R=[HT, WIN], W=[pt])
                return ps, pt
            for j, (o, m) in enumerate(DCH):
                ps, pt = fm(C_DQ + o, m)
                P.ts('dve', qst[0:m, j, :], ps[0:m, :], 32.0 ** -0.5, None, ALU.mult, R=[pt], W=[QST])
                ps, pt = fm(C_DK + o, m)
                P.cp('act', kst[0:m, j, :], ps[0:m, :], R=[pt], W=[KST])
            for j in range(2):
                ps, pt = fm(C_FQ + 128 * j, 128)
                P.ts('dve', qst[:, 3 + j, :], ps, 0.125, None, ALU.mult, R=[pt], W=[QST])
                ps, pt = fm(C_FK + 128 * j, 128)
                P.cp('act', kst[:, 3 + j, :], ps, R=[pt], W=[KST])
            P.dma('sp', SCR['qT'][:, :, t0:t0 + T], qst, R=[QST], WA=[scr_t['qT']])
            P.dma('sp', SCR['kT'][:, :, t0:t0 + T], kst, R=[KST], WA=[scr_t['kT']])
            chk(5)
            for ch in range(2):
                ps, pt = fm(C_AX + 128 * ch, 128)
                P.cp('act', axs, ps, R=[pt], W=[AXS])
                ps, pt = fm(C_AC + 128 * ch, 128)
                P.tt('dve', zext[:, ch, 2:2 + T], ps, axs, ALU.mult, R=[pt, AXS], W=[ZX])
                P.ts('pool', y1, zext[:, ch, 0:T], cwcol[:, l, 0, ch:ch + 1], None, ALU.mult, R=[ZX, CT], W=[YT])
                P.stt('dve', y2, zext[:, ch, 1:1 + T], cwcol[:, l, 1, ch:ch + 1], y1, ALU.mult, ALU.add, R=[ZX, YT, CT], W=[YT])
                P.stt('dve', y1, zext[:, ch, 2:2 + T], cwcol[:, l, 2, ch:ch + 1], y2, ALU.mult, ALU.add, R=[ZX, YT, CT], W=[YT])
                ps, pt = fm(C_AB + 128 * ch, 128)
                P.tt('dve', oast[:, ch, :], ps, y1, ALU.mult, R=[pt, YT], W=[OAS])
            P.dma('sp', SCR['obr'][0, :, :, t0:t0 + T], oast, R=[OAS], WA=[scr_t['obr']])
            if t == NT - 1:
                for ch in range(2):
                    P.dma('pool', O['p_conv'][l, s, :, ch * 128:(ch + 1) * 128].rearrange("t c -> c t"), zext[:, ch, T:T + 2], R=[ZX])
            chk(6)
            E = T + 15
            for ch in range(2):
                ps, pt = fm(C_PU + 128 * ch, 128)
                P.cp('act', uext[:, ch, 15:15 + T], ps, R=[pt], W=[UX])
                P.tt('pool', sw[:, 0, 1:E], uext[:, ch, 1:E], uext[:, ch, 0:E - 1], ALU.add, R=[UX], W=[SW])
                P.tt('pool', sw[:, 1, 3:E], sw[:, 0, 3:E], sw[:, 0, 1:E - 2], ALU.add, R=[SW], W=[SW])
                if ch == 1:
                    P.tt('pool', sw[:, 2, 7:E], sw[:, 1, 7:E], sw[:, 1, 3:E - 4], ALU.add, R=[SW], W=[SW])
                    P.tt('pool', sw[:, 3, 15:E], sw[:, 2, 15:E], sw[:, 2, 7:E - 8], ALU.add, R=[SW], W=[SW])
                for hf in range(2):
                    g = ch * 2 + hf
                    pr = slice(64 * hf, 64 * hf + 64)
                    P.stt('dve', dTt[pr, ch, :], sw[pr, g, 15:15 + T], 1.0 / POOLW[g], uext[pr, ch, 15:15 + T],
                          ALU.mult, ALU.subtract, R=[SW, UX], W=[DT])
                    if t == 0:
                        P.tt('dve', tmp16[pr, :], sw[pr, g, 15:31], C['invcnt'][pr, ch, :], ALU.mult, R=[SW, CT, DT], W=[DT])
                        P.tt('dve', dTt[pr, ch, 0:16], tmp16[pr, :], uext[pr, ch, 15:31], ALU.subtract, R=[DT, UX], W=[DT])
                ps, pt = bank([2, 3, 4, 5])
                P.mm(ps, poolW[:, l, ch, :], dTt[:, ch, :], R=[DT, CT], W=[pt])
                P.act(odst[:, ch, :], ps, AF.Identity, R=[pt, CT], W=[ODS], scale=pscol[:, l, ch:ch + 1])
            P.dma('sp', SCR['obr'][3, :, :, t0:t0 + T], odst, R=[ODS], WA=[scr_t['obr']])
            if t == NT - 1:
                for ch in range(2):
                    P.dma('pool', O['p_pool'][l, s, :, ch * 128:(ch + 1) * 128].rearrange("t c -> c t"), uext[:, ch, T:T + 15], R=[UX])
          except _Stop:
            pass
        P.barrier()
        A.pop()

    def pass_b(l, s):
        A.push()
        kT = A.get([5, S], BF16)
        KT = [Trk() for _ in range(5)]
        for j in range(5):
            P.dma('sp', kT[:, j, :], SCR['kT'][:, j, :], W=[KT[j]])
        vv = A.get([NKT, 4, 192], BF16)
        VV = [Trk() for _ in range(4)]
        vsrc = SCR['v'].rearrange("p k b a s e -> p k (b a) (s e)")
        for bp_ in range(4):
            P.dma('sp', vv[:, :, bp_, :], vsrc[:, :, bp_, :], W=[VV[bp_]])
        nF = A.get([NKT, 4], F32)
        NF = Trk()
        P.dma('sp', nF, SCR['nF'], W=[NF])
        cq = A.get([S], BF16)
        CQ = Trk()
        P.dma('sp', cq[0:4, :], SCR['cq'], W=[CQ])
        qT = A.get([2, 5, T], BF16)
        QT = [Trk(), Trk()]
        pT = A.get([3, T], BF16)
        PT = [Trk() for _ in range(3)]
        rb = A.get([2, T], F32)
        RB = [Trk(), Trk()]
        o1 = A.get([T], F32)
        t2 = A.get([T], F32)
        OT = Trk()
        odf = A.get([2, T], F32)
        ODF = [Trk(), Trk()]
        sq = A.get([T], F32)
        SQ = Trk()
        obst = A.get([2, 2, T], BF16)
        OBS = [Trk(), Trk()]
        cnt = {'a': 0, 'i': 0}

        def emit_s(it):
            (qi, t, br, h, comp, kt, nkt, ai) = it
            si = it_idx[id(it)] % 3
            ps, pt = PS[si], PST[si]
            diag = kt >= 4 * t
            j = kt - 4 * t
            hh = h % 2
            if br == 0:
                hc = 2 * h + comp
                cj, r = hc // 3, hc % 3
                P.mm(ps, kT[32 * r:32 * r + 32, cj, kt * 128:(kt + 1) * 128], qT[32 * r:32 * r + 32, qi, cj, :],
                     start=True, stop=not diag, R=[KT[cj], QT[qi]], W=[pt])
                if diag:
                    P.mm(ps, identb, C['mwd'][:, h, 384 - 128 * j:384 - 128 * j + T], start=False, stop=True, R=[CT], W=[pt])
            else:
                cj = 3 + h // 2
                P.mm(ps, kT[64 * hh:64 * hh + 64, cj, kt * 128:(kt + 1) * 128], qT[64 * hh:64 * hh + 64, qi, cj, :],
                     start=True, stop=False, R=[KT[cj], QT[qi]], W=[pt])
                P.mm(ps, C['sel4'][0:4, h, :], cq[0:4, t * T:(t + 1) * T], start=False, stop=not diag, R=[CT, CQ], W=[pt])
                if diag:
                    P.mm(ps, identb, C['mwf'][:, 384 - 128 * j:384 - 128 * j + T], start=False, stop=True, R=[CT], W=[pt])
            pi = si
            if br == 0:
                d0 = 128 * kt - 512 * t - 256
                if h == 0:
                    for half in range(2):
                        dd = d0 + 128 - 256 * half
                        di = (dd - dmin) // 128
                        P.act(pT[:, pi, 256 * half:256 * half + 256], ps[:, 256 * half:256 * half + 256], AF.Exp,
                              R=[pt, CT], W=[PT[pi]], bias=C['alibi'][:, h, di:di + 1], scale=1.0)
                else:
                    di = (d0 - dmin) // 128
                    P.act(pT[:, pi, :], ps, AF.Exp, R=[pt, CT], W=[PT[pi]], bias=C['alibi'][:, h, di:di + 1], scale=1.0)
            else:
                P.act(pT[:, pi, :], ps, AF.Exp, R=[pt, NF], W=[PT[pi]], bias=nF[:, kt, h:h + 1], scale=1.0)

        def emit_pv(it):
            (qi, t, br, h, comp, kt, nkt, ai) = it
            pi = it_idx[id(it)] % 3
            pair, hh = h // 2, h % 2
            P.mm(PS[ai], vv[:, kt, br * 2 + pair, 64 * hh:64 * hh + 128], pT[:, pi, :],
                 start=(kt == 0), stop=(kt == nkt - 1), R=[VV[br * 2 + pair], PT[pi]], W=[PST[ai]])

        def finish(l, br, h, accs):
            pair, hh = h // 2, h % 2
            orng = slice(64 * hh, 64 * hh + 64)
            drng = slice(64 * (1 - hh), 64 * (1 - hh) + 64)
            if br == 0:
                a0, a1 = accs
                P.recip(rb[orng, 0, :], PS[a0][drng, :], R=[PST[a0]], W=[RB[0]])
                P.tt('dve', o1[orng, :], PS[a0][orng, :], rb[orng, 0, :], ALU.mult, R=[PST[a0], RB[0]], W=[OT])
                P.recip(rb[orng, 1, :], PS[a1][drng, :], R=[PST[a1]], W=[RB[1]])
                P.tt('dve', t2[orng, :], PS[a1][orng, :], rb[orng, 1, :], ALU.mult, R=[PST[a1], RB[1], OT], W=[OT])
                P.stt('dve', odf[orng, pair, :], t2[orng, :], neglam[orng, l:l + 1], o1[orng, :], ALU.mult, ALU.add,
                      R=[OT, CT], W=[ODF[pair]])
                if hh == 1:
                    P.act(sq, odf[:, pair, :], AF.Square, R=[ODF[pair]], W=[SQ])
                    pm, pmt = PS[7], PST[7]
                    P.mm(pm, C['blk64'], sq, R=[SQ, CT], W=[pmt])
                    P.act(sq, pm, AF.Sqrt, R=[pmt, CT], W=[SQ], bias=epsc, scale=1.0)
                    P.recip(sq, sq, R=[SQ], W=[SQ])
                    P.stt('dve', obst[:, 0, pair, :], odf[:, pair, :], gsub[:, l:l + 1], sq, ALU.mult, ALU.mult,
                          R=[ODF[pair], SQ, CT], W=[OBS[0]])
            else:
                a0 = accs[0]
                P.recip(rb[orng, 0, :], PS[a0][drng, :], R=[PST[a0]], W=[RB[0]])
                P.tt('dve', obst[orng, 1, pair, :], PS[a0][orng, :], rb[orng, 0, :], ALU.mult, R=[PST[a0], RB[0]], W=[OBS[1]])

        it_idx = {}
        for t in range(NT):
            t0 = t * T
            qi = t % 2
            P.dma('sp', qT[:, qi], SCR['qT'][:, :, t0:t0 + T], W=[QT[qi]])
            items = []
            fin = {}
            nkt = 4 * t + 4
            for br in range(2):
                for h in range(4):
                    accs = []
                    for comp in range(2 if br == 0 else 1):
                        ai = 3 + cnt['a'] % 4
                        cnt['a'] += 1
                        accs.append(ai)
                        for kt in range(nkt):
                            items.append((qi, t, br, h, comp, kt, nkt, ai))
                    fin[len(items) - 1] = (br, h, accs)
            for it in items:
                it_idx[id(it)] = cnt['i']
                cnt['i'] += 1
            for i in range(len(items) + 1):
                if i < len(items):
                    emit_s(items[i])
                if i >= 1:
                    emit_pv(items[i - 1])
                    if (i - 1) in fin:
                        br, h, accs = fin[i - 1]
                        finish(l, br, h, accs)
                        if h == 3:
                            P.dma('sp', SCR['obr'][1 + br, :, :, t0:t0 + T], obst[:, br], R=[OBS[br]], WA=[scr_t['obr']])
            it_idx.clear()
        P.barrier()
        A.pop()

    def pass_c(l, tiles):
        A.push()
        NSLOT = 4
        ring = A.get([NSLOT, 8 * D], BF16)
        RG = [Trk() for _ in range(NSLOT)]
        pieces = [('g', i) for i in range(4)] + [('o', 0)] + [('fi', j) for j in range(6)] + [('fo', j) for j in range(3)]
        NP = len(pieces)

        def piece_src(p):
            k, j = p
            if k == 'g':
                return Wb['wg'][l, j].rearrange("p c n -> p (c n)")
            if k == 'o':
                return Wb['wo'][l].rearrange("p c n -> p (c n)")
            if k == 'fi':
                return Wb['wfi'][l, j].rearrange("p c g n -> p (c g n)")
            return Wb['wfo'][l, j].rearrange("p c n -> p (c n)")
        st = {'issued': 0}
        total = len(tiles) * NP

        def prefetch(upto):
            while st['issued'] < min(upto, total):
                i = st['issued']
                slot = i % NSLOT
                P.dma('sp', ring[:, slot, :], piece_src(pieces[i % NP]), R=[WT[l]], W=[RG[slot]])
                st['issued'] += 1

        def getp(gi, first_needed=None):
            prefetch((gi if first_needed is None else first_needed) + NSLOT)
            slot = gi % NSLOT
            return ring[:, slot, :], RG[slot]

        wbrv = A.get([4, 2, D], BF16)
        WBR = Trk()
        P.dma('sp', wbrv, Wb['wbr'][l], R=[WT[l]], W=[WBR])
        P.dma('sp', gbc[:, 0, :], I['g_norm'][l, 1].partition_broadcast(128), WA=[gbc_t])
        P.dma('sp', gbc[:, 1, :], I['g_norm'][l, 3].partition_broadcast(128), WA=[gbc_t])
        obr = A.get([4, 2, T], BF16)
        OBR = [Trk() for _ in range(4)]
        accm = A.get([8, T], F32)
        ACC = Trk()
        MT = HP
        gate = A.get([2, T], F32)
        GT = [Trk(), Trk()]
        tmpf = A.get([2, T], F32)
        TM = [Trk(), Trk()]
        aT = A.get([NFC, T], BF16)
        AT = Trk()
        ss2 = A.get([8], F32)
        rs2 = A.get([4], F32)
        S2 = Trk()
        cc = {'g': 0, 't': 0}
        CB = [2, 3, 4, 5, 6, 7]

        def out_norm_residual(lhs, lhs_t, nk, getw, gi_norm, nb):
            for b in range(nb):
                P.memset('dve', ss2[:, 2 * b:2 * b + 2], 0.0, W=[S2])
                halves = []
                for half in range(2):
                    ps, pt = bank(CB)
                    for k in range(nk):
                        w, wt = getw(k)
                        P.mm(ps, lhs[:, k, b * 128:(b + 1) * 128], w[:, half * 512:(half + 1) * 512], start=(k == 0), stop=(k == nk - 1),
                             R=[lhs_t, wt], W=[pt])
                    ti = cc['t'] % 2
                    cc['t'] += 1
                    P.act(tmpf[:, ti, :], ps, AF.Square, R=[pt], W=[TM[ti], S2], accum_out=ss2[:, 2 * b + half:2 * b + half + 1])
                    halves.append((ps, pt))
                P.tt('dve', rs2[:, b:b + 1], ss2[:, 2 * b:2 * b + 1], ss2[:, 2 * b + 1:2 * b + 2], ALU.add, R=[S2], W=[S2])
                P.act(rs2[:, b:b + 1], rs2[:, b:b + 1], AF.Sqrt, R=[S2, CT], W=[S2], bias=epsc, scale=1.0 / D)
                P.recip(rs2[:, b:b + 1], rs2[:, b:b + 1], R=[S2], W=[S2])
                for half, (ps, pt) in enumerate(halves):
                    ti = cc['t'] % 2
                    cc['t'] += 1
                    P.stt('dve', tmpf[:, ti, :], ps, rs2[:, b:b + 1], gbc[:, gi_norm, half * 512:(half + 1) * 512], ALU.mult, ALU.mult,
                          R=[pt, S2, gbc_t], W=[TM[ti]])
                    P.tt('pool', xs_t[:, b, half * 512:(half + 1) * 512], xs_t[:, b, half * 512:(half + 1) * 512], tmpf[:, ti, :], ALU.add,
                         R=[TM[ti], XS], W=[XS])

        for ti_, tl in enumerate(tiles):
            nb = tl['nb']
            Tn = nb * 128
            fs = slice(0, Tn)
            g0 = ti_ * NP
            prefetch(g0 + NSLOT)
            P.dma('sp', xs_t[:, 0:nb, :], tl['src'], R=[xres_t], W=[XS])
            P.dma('sp', hT[:, :, fs], tl['hT'], R=[scr_t['hT']], W=[HT])
            for i in range(4):
                P.dma('sp', obr[:, i, :, fs], tl['obr'][i], R=[scr_t['obr']], W=[OBR[i]])
            for i in range(4):
                wg, wgt = getp(g0 + i)
                wgv = wg.rearrange("p (c n) -> p c n", c=8)
                for m in range(8):
                    ps, pt = bank(CB)
                    for c in range(8):
                        P.mm(ps[:, fs], wgv[:, c, m * 128:(m + 1) * 128], hT[:, c, fs], start=(c == 0), stop=(c == 7), R=[wgt, HT], W=[pt])
                    gi = cc['g'] % 2
                    cc['g'] += 1
                    P.act(gate[:, gi, fs], ps[:, fs], AF.Sigmoid, R=[pt, CT], W=[GT[gi]], bias=bgcol[:, l, i, m:m + 1], scale=1.0)
                    ps2, pt2 = bank(CB)
                    for c in range(2):
                        P.mm(ps2[:, fs], wbrv[:, i, c, m * 128:(m + 1) * 128], obr[:, i, c, fs], start=(c == 0), stop=(c == 1), R=[WBR, OBR[i]], W=[pt2])
                    if i == 0:
                        P.tt('dve', accm[:, m, fs], ps2[:, fs], gate[:, gi, fs], ALU.mult, R=[pt2, GT[gi]], W=[ACC])
                    else:
                        ti = cc['t'] % 2
                        cc['t'] += 1
                        P.tt('dve', tmpf[:, ti, fs], ps2[:, fs], gate[:, gi, fs], ALU.mult, R=[pt2, GT[gi]], W=[TM[ti]])
                        if i < 3:
                            P.tt('pool', accm[:, m, fs], accm[:, m, fs], tmpf[:, ti, fs], ALU.add, R=[TM[ti], ACC], W=[ACC])
                        else:
                            P.tt('pool', mT[:, m, fs], accm[:, m, fs], tmpf[:, ti, fs], ALU.add, R=[TM[ti], ACC], W=[MT])
            wo, wot = getp(g0 + 4)
            wov = wo.rearrange("p (c n) -> p c n", c=8)
            out_norm_residual(mT, MT, 8, lambda k: (wov[:, k, :], wot), 0, nb)
            norm_to_hT(l, 2, nb=nb)
            for j in range(6):
                wf, wft = getp(g0 + 5 + j)
                wfv = wf.rearrange("p (c g n) -> p c g n", c=8, g=2)
                for jj in range(4 if j < 5 else 2):
                    fc = 4 * j + jj
                    psg, ptg = bank(CB)
                    for c in range(8):
                        P.mm(psg[:, fs], wfv[:, c, 0, jj * 128:(jj + 1) * 128], hT[:, c, fs], start=(c == 0), stop=(c == 7), R=[wft, HT], W=[ptg])
                    psu, ptu = bank(CB)
                    for c in range(8):
                        P.mm(psu[:, fs], wfv[:, c, 1, jj * 128:(jj + 1) * 128], hT[:, c, fs], start=(c == 0), stop=(c == 7), R=[wft, HT], W=[ptu])
                    gi = cc['g'] % 2
                    cc['g'] += 1
                    P.act(gate[:, gi, fs], psg[:, fs], AF.Silu, R=[ptg], W=[GT[gi]])
                    P.tt('dve', aT[:, fc, fs], psu[:, fs], gate[:, gi, fs], ALU.mult, R=[ptu, GT[gi]], W=[AT])
            wfo = [getp(g0 + 11 + j, g0 + 11) for j in range(3)]

            def getfo(k):
                w, wt = wfo[k // 8]
                return w.rearrange("p (c n) -> p c n", c=8)[:, k % 8, :], wt
            out_norm_residual(aT, AT, NFC, getfo, 1, nb)
            P.dma('pool', tl['dst'], xs_t[:, 0:nb, :], R=[XS], WA=[xres_t])
        P.barrier()
        A.pop()

    def prompt_tiles(l, s):
        last = (l == NL - 1)
        tl = []
        for t in range(NT):
            t0 = t * T
            src = I['xp'][s, t0:t0 + T, :] if l == 0 else SCR['xres'][s, t0:t0 + T, :]
            dst = O['yp'][s, t0:t0 + T, :] if last else SCR['xres'][s, t0:t0 + T, :]
            tl.append(dict(src=src.rearrange("(b p) d -> p b d", p=128), dst=dst.rearrange("(b p) d -> p b d", p=128),
                           hT=SCR['hT'][:, :, t0:t0 + T], obr=[SCR['obr'][i, :, :, t0:t0 + T] for i in range(4)], nb=4))
        return tl

    NS, PL = cfg.get('NS', 0), cfg.get('PL', 0)
    KT = PL // 128

    def pass_a_s(l):
        A.push()
        Tn = 128
        win = A.get([8, NIN], BF16)
        WIN = Trk()
        P.dma('sp', win, Wb['win'][l], R=[WT[l]], W=[WIN])
        stage = A.get([2, 512], F32)
        STG = [Trk(), Trk()]
        vst = A.get([4, 3, 64], BF16)
        vst4 = vst.rearrange("p (b a) s e -> p b a s e", b=2)
        VST = Trk()
        P.memset('pool', vst[:, :, 1, :], 1.0, W=[VST])
        kst = A.get([5, Tn], BF16)
        qst = A.get([5, Tn], BF16)
        KST, QST = Trk(), Trk()
        zext = A.get([2, NS, 18], F32)
        ZX = Trk()
        uext = A.get([2, NS, 31], F32)
        UX = Trk()
        for i in range(NS):
            for ch in range(2):
                P.dma('sp', zext[:, ch, i, 0:2], I['sconv'][l, i, :, ch * 128:(ch + 1) * 128].rearrange("t c -> c t"), WA=[ZX])
                P.dma('sp', uext[:, ch, i, 0:15], I['spool'][l, i, :, ch * 128:(ch + 1) * 128].rearrange("t c -> c t"), WA=[UX])
        axs = A.get([Tn], F32)
        AXS = Trk()
        y1 = A.get([NS, 16], F32)
        y2 = A.get([NS, 16], F32)
        YT = Trk()
        sw = A.get([4, NS, 31], F32)
        SW = Trk()
        dTt = A.get([2, Tn], BF16)
        DT = Trk()
        P.memset('dve', dTt, 0.0, W=[DT])
        oast = A.get([2, Tn], BF16)
        OAS = Trk()
        P.memset('dve', oast, 0.0, W=[OAS])
        odst = A.get([2, Tn], BF16)
        ODS = Trk()
        hlf = A.get([NS, KT, 4], F32)
        HLF = Trk()
        for i in range(NS):
            P.dma('sp', hlf[:, i], I['clf'][l, i].rearrange("(k p) h -> p k h", p=128), WA=[HLF])
        pa = A.get([KT, 4], F32)
        pb = A.get([KT, 4], F32)
        PFX = Trk()
        nFs = A.get([NS, KT + 1, 4], F32)
        NFS = Trk()
        P.memset('dve', nFs, 0.0, W=[NFS])
        ftot = A.get([NS, 4], F32)
        FT = Trk()
        lf = A.get([4], F32)
        lft = A.get([4], F32)
        lfb = A.get([4], BF16)
        LF = Trk()
        cqs = A.get([Tn], BF16)
        CQS = Trk()

        def v3(ap2d):
            return ap2d.rearrange("p (i t) -> p i t", t=32)[:, 0:NS, 0:16]

        src = I['xs'] if l == 0 else SS['xres']
        P.dma('sp', xs_t[:, 0, :], src, R=[xres_t], W=[XS])
        norm_to_hT(l, 0, nb=1)
        P.dma('sp', SS['hT'], hT[:, :, 0:Tn], R=[HT], WA=[scr_t['hT']])
        for i in range(NS):
            pf, pft = PS[7], PST[7]
            P.mm(pf[:, 0:KT * 4], C['tri'], hlf[:, i].rearrange("p k h -> p (k h)"), R=[HLF, CT], W=[pft])
            P.mm(pf[:, 128:128 + KT * 4], C['ones'], hlf[:, i].rearrange("p k h -> p (k h)"), R=[HLF, CT], W=[pft])
            P.cp('dve', pa, pf[:, 128:128 + KT * 4].rearrange("p (k h) -> p k h", h=4), R=[pft], W=[PFX])
            cur, oth = pa, pb
            sft = 1
            while sft < KT:
                P.cp('dve', oth, cur, R=[PFX], W=[PFX])
                P.tt('dve', oth[:, sft:, :], cur[:, sft:, :], cur[:, 0:KT - sft, :], ALU.add, R=[PFX], W=[PFX])
                cur, oth = oth, cur
                sft *= 2
            P.cp('dve', ftot[:, i, :], cur[:, KT - 1, :], R=[PFX], W=[FT])
            P.tt('dve', oth, cur, pf[:, 128:128 + KT * 4].rearrange("p (k h) -> p k h", h=4), ALU.subtract, R=[PFX, pft], W=[PFX])
            P.tt('dve', oth, oth, pf[:, 0:KT * 4].rearrange("p (k h) -> p k h", h=4), ALU.add, R=[PFX, pft], W=[PFX])
            P.ts('dve', nFs[:, i, 0:KT, :], oth, -1.0, None, ALU.mult, R=[PFX], W=[NFS])
        pl, plt = PS[6], PST[6]
        for gi, (c0, okn, ovn) in enumerate(((C_DK, 's_dk', 's_dv'), (C_FK, 's_fk', 's_fv'))):
            ps, pt = bank([2, 3, 4, 5])
            for c in range(8):
                P.mm(ps, hT[:, c, 0:128], win[:, c, c0:c0 + 512], start=(c == 0), stop=(c == 7), R=[HT, WIN], W=[pt])
            P.cp('act', stage[:, gi, :], ps, R=[pt], W=[STG[gi]])
            P.cp('dve', vst4[:, gi, :, 0:3:2, :], ps[:, 256:512].rearrange("p (a s e) -> p a s e", a=2, s=2), R=[pt], W=[VST])
            P.dma('pool', O[okn][l], stage[:, gi, 0:256], R=[STG[gi]])
            P.dma('pool', O[ovn][l], stage[:, gi, 256:512], R=[STG[gi]])
        for c in range(8):
            P.mm(pl[:, 0:4], hT[:, c, 0:128], win[:, c, C_FF:C_FF + 4], start=(c == 0), stop=(c == 7), R=[HT, WIN], W=[plt])
        P.dma('sp', SS['v'], vst.rearrange("p a s e -> p a (s e)"), R=[VST], WA=[scr_t['v']])
        P.tt('dve', lft, pl[:, 0:4], bfg[:, l, :], ALU.add, R=[plt, CT], W=[LF])
        P.act(lft, lft, AF.Exp, R=[LF], W=[LF], scale=-1.0)
        P.act(lft, lft, AF.Ln, R=[LF, CT], W=[LF], bias=onec, scale=1.0)
        P.ts('dve', lf, lft, -1.0, None, ALU.mult, R=[LF], W=[LF])
        P.dma('pool', O['s_lf'][l], lf, R=[LF])
        pf, pft = PS[7], PST[7]
        P.mm(pf[:, 0:4], C['bdtri'], lf, R=[LF, CT], W=[pft])
        P.cp('dve', lfb, lf, R=[LF], W=[LF])
        for i in range(NS):
            pr = slice(32 * i, 32 * i + 32)
            P.stt('dve', nFs[pr, i, KT, :], pf[pr, 0:4], -1.0, ftot[pr, i, :], ALU.mult, ALU.subtract, R=[pft, FT], W=[NFS])
            P.tt('dve', lfb[32 * i:32 * i + 1, :], lf[32 * i:32 * i + 1, :], ftot[32 * i:32 * i + 1, i, :], ALU.add, R=[LF, FT], W=[LF])
        P.dma('sp', SS['nF'], nFs, R=[NFS], WA=[scr_t['nF']])
        pc, pct = bank([2, 3, 4, 5])
        P.mm(pc[0:4, 0:128], lfb, C['bdtrib'], R=[LF, CT], W=[pct])
        P.cp('act', cqs[0:4, :], pc[0:4, 0:128], R=[pct], W=[CQS])
        P.dma('sp', SS['cq'], cqs[0:4, :], R=[CQS], WA=[scr_t['cq']])

        def fm(col0, m):
            ps, pt = bank([2, 3, 4, 5])
            for c in range(8):
                P.mm(ps[0:m, 0:Tn], win[:, c, col0:col0 + m], hT[:, c, 0:Tn], start=(c == 0), stop=(c == 7), R=[HT, WIN], W=[pt])
            return ps, pt
        for j, (o, m) in enumerate(DCH):
            ps, pt = fm(C_DQ + o, m)
            P.ts('dve', qst[0:m, j, :], ps[0:m, 0:Tn], 32.0 ** -0.5, None, ALU.mult, R=[pt], W=[QST])
            ps, pt = fm(C_DK + o, m)
            P.cp('act', kst[0:m, j, :], ps[0:m, 0:Tn], R=[pt], W=[KST])
        for j in range(2):
            ps, pt = fm(C_FQ + 128 * j, 128)
            P.ts('dve', qst[:, 3 + j, :], ps[:, 0:Tn], 0.125, None, ALU.mult, R=[pt], W=[QST])
            ps, pt = fm(C_FK + 128 * j, 128)
            P.cp('act', kst[:, 3 + j, :], ps[:, 0:Tn], R=[pt], W=[KST])
        P.dma('sp', SS['qT'], qst, R=[QST], WA=[scr_t['qT']])
        P.dma('sp', SS['kT'], kst, R=[KST], WA=[scr_t['kT']])
        for ch in range(2):
            ps, pt = fm(C_AX + 128 * ch, 128)
            P.cp('act', axs, ps[:, 0:Tn], R=[pt], W=[AXS])
            ps, pt = fm(C_AC + 128 * ch, 128)
            P.tt('dve', zext[:, ch, :, 2:18], v3(ps[:, 0:Tn]), v3(axs), ALU.mult, R=[pt, AXS], W=[ZX])
            P.ts('pool', y1, zext[:, ch, :, 0:16], cwcol[:, l, 0, ch:ch + 1], None, ALU.mult, R=[ZX, CT], W=[YT])
            P.stt('dve', y2, zext[:, ch, :, 1:17], cwcol[:, l, 1, ch:ch + 1], y1, ALU.mult, ALU.add, R=[ZX, YT, CT], W=[YT])
            P.stt('dve', y1, zext[:, ch, :, 2:18], cwcol[:, l, 2, ch:ch + 1], y2, ALU.mult, ALU.add, R=[ZX, YT, CT], W=[YT])
            ps, pt = fm(C_AB + 128 * ch, 128)
            P.tt('dve', v3(oast[:, ch, :]), v3(ps[:, 0:Tn]), y1, ALU.mult, R=[pt, YT], W=[OAS])
            for i in range(NS):
                P.dma('pool', O['s_conv'][l, i, :, ch * 128:(ch + 1) * 128].rearrange("t c -> c t"), zext[:, ch, i, 16:18], R=[ZX])
        P.dma('sp', SS['obr'][0], oast, R=[OAS], WA=[scr_t['obr']])
        for ch in range(2):
            ps, pt = fm(C_PU + 128 * ch, 128)
            P.cp('act', uext[:, ch, :, 15:31], v3(ps[:, 0:Tn]), R=[pt], W=[UX])
            P.tt('pool', sw[:, 0, :, 1:31], uext[:, ch, :, 1:31], uext[:, ch, :, 0:30], ALU.add, R=[UX], W=[SW])
            P.tt('pool', sw[:, 1, :, 3:31], sw[:, 0, :, 3:31], sw[:, 0, :, 1:29], ALU.add, R=[SW], W=[SW])
            if ch == 1:
                P.tt('pool', sw[:, 2, :, 7:31], sw[:, 1, :, 7:31], sw[:, 1, :, 3:27], ALU.add, R=[SW], W=[SW])
                P.tt('pool', sw[:, 3, :, 15:31], sw[:, 2, :, 15:31], sw[:, 2, :, 7:23], ALU.add, R=[SW], W=[SW])
            for hf in range(2):
                g = ch * 2 + hf
                pr = slice(64 * hf, 64 * hf + 64)
                P.stt('dve', v3(dTt[:, ch, :])[pr], sw[pr, g, :, 15:31], 1.0 / POOLW[g], uext[pr, ch, :, 15:31],
                      ALU.mult, ALU.subtract, R=[SW, UX], W=[DT])
            ps, pt = bank([2, 3, 4, 5])
            P.mm(ps[:, 0:Tn], poolW[:, l, ch, :], dTt[:, ch, :], R=[DT, CT], W=[pt])
            P.act(odst[:, ch, :], ps[:, 0:Tn], AF.Identity, R=[pt, CT], W=[ODS], scale=pscol[:, l, ch:ch + 1])
            for i in range(NS):
                P.dma('pool', O['s_pool'][l, i, :, ch * 128:(ch + 1) * 128].rearrange("t c -> c t"), uext[:, ch, i, 16:31], R=[UX])
        P.dma('sp', SS['obr'][3], odst, R=[ODS], WA=[scr_t['obr']])
        P.barrier()
        A.pop()

    def pass_b_s(l):
        A.push()
        NK = KT + 1
        kTs = A.get([5, PL + 128], BF16)
        KTS = Trk()
        vvs = A.get([NK, 4, 192], BF16)
        VVS = Trk()
        P.memset('pool', vvs.rearrange("p k a (s e) -> p (k a) s e", s=3)[:, :, 1, :], 1.0, W=[VVS])
        nFs = A.get([NS, NK, 4], F32)
        NFS = Trk()
        P.dma('sp', nFs, SS['nF'], W=[NFS])
        cqs = A.get([128], BF16)
        CQS = Trk()
        P.dma('sp', cqs[0:4, :], SS['cq'], W=[CQS])
        qTs = A.get([5, 128], BF16)
        QTS = Trk()
        P.dma('sp', qTs, SS['qT'], W=[QTS])
        GK = 4
        stg = A.get([3, GK, 256], F32)
        STG = [Trk() for _ in range(3)]
        pT = A.get([2, KT * 16], BF16)
        PTt = [Trk(), Trk()]
        pTn = A.get([2, 16], BF16)
        PTN = [Trk(), Trk()]
        rb = A.get([2, 16], F32)
        RB = [Trk(), Trk()]
        o1 = A.get([16], F32)
        t2 = A.get([16], F32)
        OT = Trk()
        odf = A.get([2, 16], F32)
        ODF = [Trk(), Trk()]
        sq = A.get([16], F32)
        SQ = Trk()
        obst = A.get([2, 2, 128], BF16)
        OBS = Trk()
        P.memset('dve', obst, 0.0, W=[OBS])
        qm = A.get([12, 128], BF16)
        QM = Trk()
        P.memset('dve', qm, 0.0, W=[QM])
        for hc_ in range(8):
            cj_, r_ = hc_ // 3, hc_ % 3
            P.cp('dve', qm[32 * r_:32 * r_ + 32, hc_, :], qTs[32 * r_:32 * r_ + 32, cj_, :], R=[QTS], WA=[QM])
        for h_ in range(4):
            hh_ = h_ % 2
            P.cp('dve', qm[64 * hh_:64 * hh_ + 64, 8 + h_, :], qTs[64 * hh_:64 * hh_ + 64, 3 + h_ // 2, :], R=[QTS], WA=[QM])
        cnt = {'g': 0, 's': 0, 'a': 0, 'p': 0}
        bstop = cfg.get('bstop', 99)
        for i in range(NS):
          try:
            P.dma('sp', kTs[:, :, PL:PL + 128], SS['kT'], W=[KTS])
            P.dma('sp', vvs[:, KT, :, :], SS['v'], WA=[VVS])
            for br, (kn, vn) in enumerate((('cdk', 'cdv'), ('cfk', 'cfv'))):
                chunks = DCH if br == 0 else [(0, 128), (128, 128)]
                for g in range(KT // GK):
                    si = cnt['g'] % 3
                    cnt['g'] += 1
                    P.dma('sp', stg[:, si], I[kn][l, i, g * GK * 128:(g + 1) * GK * 128, :].rearrange("(k p) f -> p k f", p=128), W=[STG[si]])
                    for j, (o, m) in enumerate(chunks):
                        ps, pt = bank([0, 1, 2])
                        for k in range(GK):
                            P.tr(ps[0:m, k * 128:(k + 1) * 128], stg[:, si, k, o:o + m], identf, R=[STG[si], CT], W=[pt], tick=(k == GK - 1))
                        cj = j if br == 0 else 3 + j
                        P.cp('act' if j % 2 else 'dve', kTs[0:m, cj, g * GK * 128:(g + 1) * GK * 128], ps[0:m, 0:GK * 128], R=[pt], WA=[KTS])
                    si = cnt['g'] % 3
                    cnt['g'] += 1
                    P.dma('sp', stg[:, si], I[vn][l, i, g * GK * 128:(g + 1) * GK * 128, :].rearrange("(k p) f -> p k f", p=128), W=[STG[si]])
                    for k in range(GK):
                        P.cp('pool', vvs[:, g * GK + k, 2 * br:2 * br + 2, :].rearrange("p a (s e) -> p a s e", s=3)[:, :, 0:3:2, :],
                             stg[:, si, k].rearrange("p (a s e) -> p a s e", a=2, s=2), R=[STG[si]], WA=[VVS])
            qs = slice(32 * i, 32 * i + 16)
            if bstop == 1:
                raise _Stop()
            for br in range(2):
                if bstop == 2 and br == 1:
                    raise _Stop()
                for h in range(4):
                    pair, hh = h // 2, h % 2
                    orng = slice(64 * hh, 64 * hh + 64)
                    drng = slice(64 * (1 - hh), 64 * (1 - hh) + 64)
                    accs = []
                    for comp in range(2 if br == 0 else 1):
                        ai = 5 + cnt['a'] % 2
                        cnt['a'] += 1
                        accs.append(ai)
                        acc, acct = PS[ai], PST[ai]
                        ps, pt = bank([3, 4])
                        pi = cnt['p'] % 2
                        cnt['p'] += 1
                        if br == 0:
                            hc = 2 * h + comp
                            cj = hc // 3
                            krows = slice(0, DCH[cj][1])
                            qsel = hc
                        else:
                            cj = 3 + h // 2
                            krows = slice(0, 128)
                            qsel = 8 + h
                        for kt in range(KT):
                            P.mm(ps[:, kt * 16:(kt + 1) * 16], kTs[krows, cj, kt * 128:(kt + 1) * 128], qm[krows, qsel, qs],
                                 start=(kt == 0), stop=False, R=[KTS, QM], W=[pt], tick=False)
                        if br == 0:
                            P.mm(ps[:, 0:KT * 16], C['onesrow'][0:1, :], C['ktb'][0:1, h].rearrange("p k q -> p (k q)"),
                                 start=False, stop=True, R=[CT], W=[pt])
                            di0 = (0 - dmin) // 128
                            P.act(pT[:, pi, :], ps[:, 0:KT * 16], AF.Exp, R=[pt, CT], W=[PTt[pi]], bias=C['alibi'][:, h, di0:di0 + 1], scale=1.0)
                        else:
                            P.mm(ps[:, 0:KT * 16].rearrange("p (k q) -> p k q", q=16), C['sel4'][0:4, h, :],
                                 cqs[0:4, qs].unsqueeze(1).to_broadcast([4, KT, 16]), start=False, stop=True, R=[CT, CQS], W=[pt])
                            for kt in range(KT):
                                P.act(pT[:, pi, kt * 16:(kt + 1) * 16], ps[:, kt * 16:(kt + 1) * 16], AF.Exp, R=[pt, NFS], W=[PTt[pi]],
                                      bias=nFs[:, i, kt, h:h + 1], scale=1.0)
                        if bstop == 3:
                            raise _Stop()
                        psn, ptn = PS[7], PST[7]
                        P.mm(psn[:, 0:16], kTs[krows, cj, PL:PL + 128], qm[krows, qsel, qs], start=True, stop=False, R=[KTS, QM], W=[ptn], tick=False)
                        if br == 0:
                            P.mm(psn[:, 0:16], identb, C['mnd'][:, h, i, :], start=False, stop=True, R=[CT], W=[ptn])
                            P.act(pTn[:, pi, :], psn[:, 0:16], AF.Exp, R=[ptn], W=[PTN[pi]])
                        else:
                            P.mm(psn[:, 0:16], C['sel4'][0:4, h, :], cqs[0:4, qs], start=False, stop=False, R=[CT, CQS], W=[ptn], tick=False)
                            P.mm(psn[:, 0:16], identb, C['mnf'][:, i, :], start=False, stop=True, R=[CT], W=[ptn])
                            P.act(pTn[:, pi, :], psn[:, 0:16], AF.Exp, R=[ptn, NFS], W=[PTN[pi]], bias=nFs[:, i, KT, h:h + 1], scale=1.0)
                        if bstop == 4:
                            raise _Stop()
                        for kt in range(KT):
                            P.mm(acc[:, 0:16], vvs[:, kt, br * 2 + pair, 64 * hh:64 * hh + 128], pT[:, pi, kt * 16:(kt + 1) * 16],
                                 start=(kt == 0), stop=False, R=[VVS, PTt[pi]], W=[acct], tick=False)
                        P.mm(acc[:, 0:16], vvs[:, KT, br * 2 + pair, 64 * hh:64 * hh + 128], pTn[:, pi, :],
                             start=False, stop=True, R=[VVS, PTN[pi]], W=[acct])
                        if bstop == 6:
                            raise _Stop()
                    if bstop == 5:
                        raise _Stop()
                    if br == 0:
                        a0, a1 = accs
                        P.recip(rb[orng, 0, :], PS[a0][drng, 0:16], R=[PST[a0]], W=[RB[0]])
                        P.tt('dve', o1[orng, :], PS[a0][orng, 0:16], rb[orng, 0, :], ALU.mult, R=[PST[a0], RB[0]], W=[OT])
                        P.recip(rb[orng, 1, :], PS[a1][drng, 0:16], R=[PST[a1]], W=[RB[1]])
                        P.tt('dve', t2[orng, :], PS[a1][orng, 0:16], rb[orng, 1, :], ALU.mult, R=[PST[a1], RB[1], OT], W=[OT])
                        P.stt('dve', odf[orng, pair, :], t2[orng, :], neglam[orng, l:l + 1], o1[orng, :], ALU.mult, ALU.add,
                              R=[OT, CT], W=[ODF[pair]])
                        if hh == 1:
                            P.act(sq, odf[:, pair, :], AF.Square, R=[ODF[pair]], W=[SQ])
                            pm, pmt = PS[7], PST[7]
                            P.mm(pm[:, 16:32], C['blk64'], sq, R=[SQ, CT], W=[pmt])
                            P.act(sq, pm[:, 16:32], AF.Sqrt, R=[pmt, CT], W=[SQ], bias=epsc, scale=1.0)
                            P.recip(sq, sq, R=[SQ], W=[SQ])
                            P.stt('dve', obst[:, 0, pair, qs], odf[:, pair, :], gsub[:, l:l + 1], sq, ALU.mult, ALU.mult,
                                  R=[ODF[pair], SQ, CT], W=[OBS])
                    else:
                        a0 = accs[0]
                        P.recip(rb[orng, 0, :], PS[a0][drng, 0:16], R=[PST[a0]], W=[RB[0]])
                        P.tt('dve', obst[orng, 1, pair, qs], PS[a0][orng, 0:16], rb[orng, 0, :], ALU.mult, R=[PST[a0], RB[0]], W=[OBS])
          except _Stop:
            pass
        for br in range(2):
            P.dma('sp', SS['obr'][1 + br], obst[:, br], R=[OBS], WA=[scr_t['obr']])
        P.barrier()
        A.pop()

    def sample_tiles(l):
        last = (l == NL - 1)
        src = I['xs'] if l == 0 else SS['xres']
        dst = O['ys'] if last else SS['xres']
        return [dict(src=src.rearrange("(b p) d -> p b d", p=128), dst=dst.rearrange("(b p) d -> p b d", p=128),
                     hT=SS['hT'], obr=[SS['obr'][i] for i in range(4)], nb=1)]

    for l in range(NL):
        if l > 0:
            convert_layer(l)
            P.barrier(final=True)
        for s in range(NSEQ):
            if 'A' in cfg.get('passes', 'ABC'):
                pass_a(l, s)
            if 'B' in cfg.get('passes', 'ABC'):
                pass_b(l, s)
            if 'C' in cfg.get('passes', 'ABC'):
                pass_c(l, prompt_tiles(l, s))
        if NS:
            sp = cfg.get('spass', 'ABC')
            if 'A' in sp:
                pass_a_s(l)
            if 'B' in sp:
                pass_b_s(l)
            if 'C' in sp:
                pass_c(l, sample_tiles(l))


_CACHE = {}

WNAMES = ('g_norm', 'w_in', 'b_forget', 'conv_w', 'lambda_qk', 'diff_subln', 'pool_w', 'pool_scale', 'w_branch', 'w_gate',
          'b_gate', 'w_out', 'w_ffn_in', 'w_ffn_out')


def run_all(cfg, inp, ncore=8):
    key = tuple(sorted((k, v) for k, v in cfg.items() if not k.startswith('_')))
    if key not in _CACHE:
        _CACHE[key] = build(cfg)
    nc = _CACHE[key]
    NL, NSEQ, S, NS, PL = cfg['NL'], cfg['NSEQ'], cfg['S'], cfg.get('NS', 0), cfg.get('PL', 0)
    consts = make_consts(S, NS, PL)
    consts.pop('_dmin')
    f32 = lambda a: np.asarray(a, np.float32)
    x_prompt = f32(inp['x_prompt'])
    wts = {w: np.ascontiguousarray(f32(inp[w])) for w in WNAMES}
    in_maps = []
    for c in range(ncore):
        m = {'xp': np.ascontiguousarray(x_prompt[c * NSEQ:(c + 1) * NSEQ])}
        m.update(wts)
        for k, v in consts.items():
            m['c_' + k] = v
        if NS:
            xs = f32(inp['x_sample'])
            DS = xs.shape[1]
            pad = np.zeros((128, D), np.float32)
            for i in range(NS):
                pad[32 * i:32 * i + DS] = xs[c * NS + i]
            m['xs'] = pad
            sl = slice(c * NS, (c + 1) * NS)
            m['cdk'] = np.ascontiguousarray(f32(inp['cache_diff_k'])[:, sl].reshape(NL, NS, PL, 256))
            m['cdv'] = np.ascontiguousarray(f32(inp['cache_diff_v'])[:, sl].reshape(NL, NS, PL, 256))
            m['cfk'] = np.ascontiguousarray(f32(inp['cache_fox_k'])[:, sl].reshape(NL, NS, PL, 256))
            m['cfv'] = np.ascontiguousarray(f32(inp['cache_fox_v'])[:, sl].reshape(NL, NS, PL, 256))
            m['clf'] = np.ascontiguousarray(f32(inp['cache_fox_lf'])[:, sl])
            m['sconv'] = np.ascontiguousarray(f32(inp['state_conv'])[:, sl])
            m['spool'] = np.ascontiguousarray(f32(inp['state_pool'])[:, sl])
        in_maps.append(m)
    res = run_bass_kernel_spmd(nc, in_maps, core_ids=list(range(ncore))).results
    B = NSEQ * ncore
    o = {}
    o['yp'] = np.concatenate([r['yp'] for r in res], axis=0)
    cat1 = lambda nm: np.concatenate([r[nm] for r in res], axis=1)
    o['p_dk'] = cat1('p_dk').reshape(NL, B, S, 4, 2, 32)
    o['p_dv'] = cat1('p_dv').reshape(NL, B, S, 4, 64)
    o['p_fk'] = cat1('p_fk').reshape(NL, B, S, 4, 64)
    o['p_fv'] = cat1('p_fv').reshape(NL, B, S, 4, 64)
    o['p_lf'] = cat1('p_lf')
    o['p_conv'] = cat1('p_conv')
    o['p_pool'] = cat1('p_pool')
    if NS:
        DS = np.asarray(inp['x_sample']).shape[1]
        rows = np.concatenate([np.arange(32 * i, 32 * i + DS) for i in range(NS)])
        o['ys'] = np.concatenate([r['ys'][rows].reshape(NS, DS, D) for r in res], axis=0)

        def tok(nm, shp):
            return np.concatenate([r[nm][:, rows].reshape((NL, NS, DS) + shp) for r in res], axis=1)
        o['s_dk'] = tok('s_dk', (4, 2, 32))
        o['s_dv'] = tok('s_dv', (4, 64))
        o['s_fk'] = tok('s_fk', (4, 64))
        o['s_fv'] = tok('s_fv', (4, 64))
        o['s_lf'] = tok('s_lf', (4,))
        o['s_conv'] = cat1('s_conv')
        o['s_pool'] = cat1('s_pool')
    if cfg.get('debug'):
        for nm in ('obrs_scr',):
            if nm in res[0]:
                o['dbg_' + nm] = np.asarray(res[0][nm]).astype(np.float32)
    return o


ONAMES = ('yp', 'ys', 'p_dk', 'p_dv', 'p_fk', 'p_fv', 'p_lf', 'p_conv', 'p_pool', 's_dk', 's_dv', 's_fk', 's_fv', 's_lf', 's_conv', 's_pool')


def kernel(**inp):
    B, S, _ = inp['x_prompt'].shape
    NL = inp['g_norm'].shape[0]
    DB = inp['x_sample'].shape[0]
    PL = inp['cache_diff_k'].shape[2]
    cfg = dict(NL=NL, NSEQ=B // 8, S=S, NS=DB // 8, PL=PL)
    import os
    if os.environ.get('BSTOP'):
        cfg['bstop'] = int(os.environ['BSTOP'])
    if os.environ.get('SPASS'):
        cfg['spass'] = os.environ['SPASS']
    if os.environ.get('PPASS') is not None:
        cfg['passes'] = os.environ['PPASS']
    o = run_all(cfg, inp)
    return tuple(np.ascontiguousarray(o[k], dtype=np.float32) for k in ONAMES)
```

```python
import numpy as np
import concourse.bass as bass
import concourse.mybir as mybir
from concourse.bass_utils import run_bass_kernel_spmd

F32 = mybir.dt.float32
BF16 = mybir.dt.bfloat16
ALU = mybir.AluOpType
AF = mybir.ActivationFunctionType

D = 1024
NIN = 2564
DFF = 2816
NFC = 22
SLOPES = [2.0 ** (-8.0 * (h + 1) / 4) for h in range(4)]
NEG = -30000.0
POOLW = (2, 4, 8, 16)
C_AX, C_AB, C_AC, C_DQ, C_DK, C_DV, C_FQ, C_FK, C_FV, C_FF, C_PU = 0, 256, 512, 768, 1024, 1280, 1536, 1792, 2048, 2304, 2308
DCH = [(0, 96), (96, 96), (192, 64)]

ENGS = ('pe', 'act', 'dve', 'pool', 'sp')


class Trk:
    __slots__ = ('w', 'r', 'excl', 'wb')

    def __init__(self, excl=False):
        self.w = {}
        self.wb = {}
        self.r = {}
        self.excl = excl


class Prog:
    def __init__(self, ndma=20):
        self.q = {e: [] for e in ENGS}
        self.tick = {e: 0 for e in ENGS}
        self.seen = {e: {} for e in ENGS}
        self.ndma = ndma
        self.dcnt = {}
        self.dnext = {e: 0 for e in ENGS}
        self.samewait = True

    def _need(self, eng, waits, ev):
        if ev is None:
            return
        k, v = ev
        if k == eng and (eng == 'pe' or not self.samewait):
            return
        if self.seen[eng].get(k, 0) >= v:
            return
        if waits.get(k, 0) < v:
            waits[k] = v

    def _deps(self, eng, R, W, WA=()):
        waits = {}
        for b in R:
            for k, v in b.w.items():
                self._need(eng, waits, (k, v))
        for b in W:
            for k, v in b.w.items():
                self._need(eng, waits, (k, v))
            for k, v in b.r.items():
                self._need(eng, waits, (k, v))
        for b in WA:
            for k, v in b.wb.items():
                self._need(eng, waits, (k, v))
            for k, v in b.r.items():
                self._need(eng, waits, (k, v))
        for k, v in waits.items():
            self.seen[eng][k] = v
            self.q[eng].append(('wait', k, v))

    def _mark(self, ev, R, W, WA=()):
        k, v = ev
        for b in R:
            if b.r.get(k, 0) < v:
                b.r[k] = v
        for b in W:
            b.w = {k: v}
            b.wb = {k: v}
            b.r = {}
        for b in WA:
            if b.w.get(k, 0) < v:
                b.w[k] = v

    def op(self, eng, fn, R=(), W=(), tick=True, WA=()):
        if any(b.excl for b in R):
            W = list(W) + [b for b in R if b.excl and b not in W]
            R = [b for b in R if not b.excl]
        self._deps(eng, R, W, WA)
        if tick:
            self.tick[eng] += 1
            ev = (eng, self.tick[eng])
            self.q[eng].append(('op', fn, (eng, 1)))
        else:
            ev = (eng, self.tick[eng] + 1)
            self.q[eng].append(('op', fn, None))
        self._mark(ev, R, W, WA)

    def dma(self, qe, out, in_, R=(), W=(), WA=(), pool=None, **kw):
        pl = pool or qe
        i = self.dnext.get(pl, 0)
        self.dnext[pl] = (i + 1) % self.ndma
        key = ('d', pl, i)
        n = self.dcnt.get(key, 0)
        waits = {}
        if n > 0:
            self._need(qe, waits, (key, 16 * n))
        for k, v in waits.items():
            self.seen[qe][k] = v
            self.q[qe].append(('wait', k, v))
        self._deps(qe, R, W, WA)
        self.dcnt[key] = n + 1
        ev = (key, 16 * (n + 1))
        self.q[qe].append(('op', (lambda e: e.dma_start(out=out, in_=in_, **kw)), (key, 16)))
        self._mark(ev, R, W, WA)

    def barrier(self, final=False):
        evs = [(e, self.tick[e]) for e in ('pe', 'act', 'dve', 'pool') if self.tick[e] > 0]
        evs += [(k, 16 * n) for k, n in self.dcnt.items() if final or k[1] != 'cv']
        for eng in ENGS:
            waits = {}
            for ev in evs:
                if ev[0] == eng:
                    continue
                self._need(eng, waits, ev)
            for k, v in waits.items():
                self.seen[eng][k] = v
                self.q[eng].append(('wait', k, v))

    def mm(self, out, lhsT, rhs, start=True, stop=True, R=(), W=(), tick=None):
        if tick is None:
            tick = stop
        self.op('pe', lambda e: e.matmul(out, lhsT=lhsT, rhs=rhs, start=start, stop=stop), R, W, tick)

    def tr(self, out, in_, ident, R=(), W=(), tick=True):
        self.op('pe', lambda e: e.transpose(out, in_, ident), R, W, tick)

    def act(self, out, in_, func, R=(), W=(), **kw):
        self.op('act', lambda e: e.activation(out, in_, func, **kw), R, W)

    def tt(self, eng, out, in0, in1, op, R=(), W=()):
        self.op(eng, lambda e: e.tensor_tensor(out, in0, in1, op), R, W)

    def ts(self, eng, out, in0, s1, s2, op0, op1=None, R=(), W=()):
        if op1 is None:
            self.op(eng, lambda e: e.tensor_scalar(out, in0, s1, None, op0), R, W)
        else:
            self.op(eng, lambda e: e.tensor_scalar(out, in0, s1, s2, op0, op1), R, W)

    def stt(self, eng, out, in0, scalar, in1, op0, op1, R=(), W=()):
        self.op(eng, lambda e: e.scalar_tensor_tensor(out, in0, scalar, in1, op0, op1), R, W)

    def cp(self, eng, out, in_, R=(), W=(), WA=()):
        if eng == 'act':
            self.op(eng, lambda e: e.activation(out, in_, AF.Copy), R, W, True, WA)
        else:
            self.op(eng, lambda e: e.tensor_copy(out, in_), R, W, True, WA)

    def memset(self, eng, ap, val, W=()):
        self.op(eng, lambda e: e.memset(ap, val), (), W)

    def recip(self, out, in_, R=(), W=()):
        self.op('dve', lambda e: e.reciprocal(out, in_), R, W)


class Arena:
    def __init__(self, ap_f32, nbytes):
        self.ap = ap_f32
        self.n = nbytes
        self.off = 0
        self.marks = []

    def push(self):
        self.marks.append(self.off)

    def pop(self):
        self.off = self.marks.pop()

    def get(self, free_shape, dt, parts=128):
        esz = 4 if dt == F32 else 2
        n = int(np.prod(free_shape))
        nb = (n * esz + 63) // 64 * 64
        assert self.off + nb <= self.n, ("SBUF arena overflow", self.off, nb, self.n)
        a = self.ap[:, self.off // 4:(self.off + nb) // 4]
        self.off += nb
        if dt != F32:
            a = a.bitcast(dt)
        a = a[:, 0:n]
        if len(free_shape) > 1:
            names = ' '.join('a%d' % i for i in range(len(free_shape)))
            kw = {'a%d' % i: int(free_shape[i]) for i in range(1, len(free_shape))}
            a = a.rearrange('p (%s) -> p %s' % (names, names), **kw)
        return a


def make_consts(S, NS=0, PL=0):
    c = {}
    c['ident'] = np.eye(128, dtype=np.float32)
    kk = np.arange(128)[:, None]
    c['tri'] = (kk <= np.arange(128)[None, :]).astype(np.float32)
    c['ones'] = np.ones((128, 128), np.float32)
    c['blk64'] = ((kk // 64) == (np.arange(128)[None, :] // 64)).astype(np.float32) / 64.0
    rel = np.arange(896)[None, :] - 384
    allowed = (kk // 64) <= np.floor_divide(rel, 64)
    md = np.zeros((128, 4, 896), np.float32)
    for h in range(4):
        v = np.where(kk <= rel, 0.0, -2.0 * SLOPES[h] * (kk - rel))
        md[:, h, :] = np.where(allowed, v, NEG)
    c['mwd'] = md
    c['mwf'] = np.where(kk <= rel, 0.0, NEG).astype(np.float32)
    c['twide'] = (kk <= rel).astype(np.float32)
    NQT = S // 512
    dmin = -512 * (NQT - 1) - 256 - 128
    ndi = (256 - dmin) // 128 + 1
    al = np.zeros((128, 4, ndi), np.float32)
    for h in range(4):
        for di in range(ndi):
            al[:, h, di] = SLOPES[h] * (np.arange(128) + dmin + 128 * di)
    c['alibi'] = al
    c['_dmin'] = dmin
    sel = np.zeros((128, 4, 128), np.float32)
    for h in range(4):
        sel[h, h, :] = 1.0
    c['sel4'] = sel
    ic = np.zeros((128, 2, 16), np.float32)
    for ch in range(2):
        for p in range(128):
            w = POOLW[ch * 2 + p // 64]
            ic[p, ch, :] = 1.0 / np.minimum(w, np.arange(16) + 1)
    c['invcnt'] = ic
    if NS:
        idx = np.arange(128)
        blk, tt_ = idx // 32, idx % 32
        same = (blk[:, None] == blk[None, :])
        c['bdtri'] = (same & (tt_[:, None] <= tt_[None, :])).astype(np.float32)
        mnd = np.full((128, 4, NS, 16), NEG, np.float32)
        mnf = np.full((128, NS, 16), NEG, np.float32)
        for i in range(NS):
            for tk in range(16):
                k = 32 * i + tk
                tq = np.arange(16)
                for h in range(4):
                    mnd[k, h, i, :] = SLOPES[h] * (tq - np.abs(tq - tk))
                mnf[k, i, :] = np.where(tk <= tq, 0.0, NEG)
        c['mnd'] = mnd
        c['mnf'] = mnf
        KT = PL // 128
        kb = np.zeros((1, 4, KT, 16), np.float32)
        for h in range(4):
            for kt in range(KT):
                kb[0, h, kt, :] = SLOPES[h] * (128 * kt - PL)
        c['ktb'] = kb
        c['bdtrib'] = c['bdtri']
        c['onesrow'] = np.ones((1, 128), np.float32)
    return c


CONST_BF = ('ident', 'mwd', 'mwf', 'twide', 'sel4', 'mnd', 'mnf', 'ktb', 'onesrow', 'bdtrib')


def build(cfg):
    NL, NSEQ, S = cfg['NL'], cfg['NSEQ'], cfg['S']
    T = 512
    NT = S // T
    NKT = S // 128
    consts = make_consts(S, cfg.get('NS', 0), cfg.get('PL', 0))
    dmin = consts.pop('_dmin')
    nc = bass.Bass("TRN2", target_bir_lowering=False)

    def din(name, shape, dt=F32):
        return nc.dram_tensor(name, list(shape), dt, kind="ExternalInput").ap()

    def dout(name, shape, dt=F32):
        return nc.dram_tensor(name, list(shape), dt, kind="ExternalOutput").ap()

    def dint(name, shape, dt=F32):
        return nc.dram_tensor(name, list(shape), dt, kind="Internal").ap()

    NS, PL = cfg.get('NS', 0), cfg.get('PL', 0)
    KT = PL // 128
    I = {}
    I['xp'] = din('xp', [NSEQ, S, D])
    if NS:
        I['xs'] = din('xs', [128, D])
        for nm in ('cdk', 'cdv', 'cfk', 'cfv'):
            I[nm] = din(nm, [NL, NS, PL, 256])
        I['clf'] = din('clf', [NL, NS, PL, 4])
        I['sconv'] = din('sconv', [NL, NS, 2, 256])
        I['spool'] = din('spool', [NL, NS, 15, 256])
    for nm, shp in (('g_norm', [NL, 4, D]), ('w_in', [NL, D, NIN]), ('b_forget', [NL, 4]), ('conv_w', [NL, 3, 256]),
                    ('lambda_qk', [NL, 4, 32]), ('diff_subln', [NL, 64]), ('pool_w', [NL, 4, 64, 64]),
                    ('pool_scale', [NL, 256]), ('w_branch', [NL, 4, 256, D]), ('w_gate', [NL, 4, D, D]),
                    ('b_gate', [NL, 4, D]), ('w_out', [NL, D, D]), ('w_ffn_in', [NL, D, 2 * DFF]),
                    ('w_ffn_out', [NL, DFF, D])):
        I[nm] = din(nm, shp)
    CI = {k: din('c_' + k, list(v.shape)) for k, v in consts.items()}

    O = {}
    O['yp'] = dout('yp', [NSEQ, S, D])
    for nm in ('p_dk', 'p_dv', 'p_fk', 'p_fv'):
        O[nm] = dout(nm, [NL, NSEQ, S, 256])
    O['p_lf'] = dout('p_lf', [NL, NSEQ, S, 4])
    O['p_conv'] = dout('p_conv', [NL, NSEQ, 2, 256])
    O['p_pool'] = dout('p_pool', [NL, NSEQ, 15, 256])
    if NS:
        O['ys'] = dout('ys', [128, D])
        for nm in ('s_dk', 's_dv', 's_fk', 's_fv'):
            O[nm] = dout(nm, [NL, 128, 256])
        O['s_lf'] = dout('s_lf', [NL, 128, 4])
        O['s_conv'] = dout('s_conv', [NL, NS, 2, 256])
        O['s_pool'] = dout('s_pool', [NL, NS, 15, 256])

    Wb = {}
    Wb['win'] = dint('wb_in', [NL, 128, 8, NIN], BF16)
    Wb['wg'] = dint('wb_g', [NL, 4, 128, 8, D], BF16)
    Wb['wbr'] = dint('wb_br', [NL, 128, 4, 2, D], BF16)
    Wb['wo'] = dint('wb_o', [NL, 128, 8, D], BF16)
    Wb['wfi'] = dint('wb_fi', [NL, 6, 128, 8, 2, 512], BF16)
    Wb['wfo'] = dint('wb_fo', [NL, 3, 128, 8, D], BF16)
    dbg = dout if cfg.get('debug') else dint
    xres = dbg('xres', [NSEQ, S, D])
    hT_scr = dbg('hT_scr', [128, 8, S], BF16)
    qT_scr = dbg('qT_scr', [128, 5, S], BF16)
    kT_scr = dbg('kT_scr', [128, 5, S], BF16)
    v_scr = dbg('v_scr', [128, NKT, 2, 2, 3, 64], BF16)
    nF_scr = dbg('nF_scr', [128, NKT, 4])
    cq_scr = dbg('cq_scr', [4, S], BF16)
    obr_scr = dbg('obr_scr', [4, 128, 2, S], BF16)
    SS = {}
    if NS:
        SS = dict(xres=dint('xres_s', [128, D]), hT=dint('hTs_scr', [128, 8, 128], BF16), qT=dint('qTs_scr', [128, 5, 128], BF16),
                  kT=dint('kTs_scr', [128, 5, 128], BF16), v=dint('vs_scr', [128, 4, 192], BF16),
                  nF=dint('nFs_scr', [128, NS, KT + 1, 4]), cq=dint('cqs_scr', [4, 128], BF16),
                  obr=dbg('obrs_scr', [4, 128, 2, 128], BF16))

    P = Prog()
    NARENA = 206 * 1024

    with nc.sbuf_tensor("arena", [128, NARENA // 4], F32) as arena_t:
        psum_ts = []
        import contextlib
        with contextlib.ExitStack() as es:
            for i in range(8):
                psum_ts.append(es.enter_context(nc.psum_tensor("ps%d" % i, [128, 512], F32)))
            sems = {}
            for e in ('pe', 'act', 'dve', 'pool'):
                sems[e] = es.enter_context(nc.semaphore("tick_" + e))
            for qe in ('sp', 'pool', 'cv'):
                for i in range(P.ndma):
                    sems[('d', qe, i)] = es.enter_context(nc.semaphore("dma_%s_%d" % (qe, i)))

            A = Arena(arena_t[:], NARENA)
            PS = [t[:] for t in psum_ts]
            PST = [Trk(excl=True) for _ in range(8)]

            gen_program(nc, P, A, PS, PST, cfg, I, CI, O, Wb, consts, dmin,
                        dict(xres=xres, hT=hT_scr, qT=qT_scr, kT=kT_scr, v=v_scr, nF=nF_scr, cq=cq_scr, obr=obr_scr), SS)

            P.barrier(final=True)

            with nc.allow_non_contiguous_dma(reason="small strided parameter / state transfers"), nc.Block() as block:
                def replay(e, items):
                    for it in items:
                        if it[0] == 'wait':
                            e.wait_ge(sems[it[1]], it[2])
                        else:
                            ins = it[1](e)
                            if it[2] is not None:
                                ins.then_inc(sems[it[2][0]], it[2][1])

                @block.tensor
                def _(e):
                    replay(e, P.q['pe'])

                @block.scalar
                def _(e):
                    replay(e, P.q['act'])

                @block.vector
                def _(e):
                    replay(e, P.q['dve'])

                @block.gpsimd
                def _(e):
                    replay(e, P.q['pool'])

                @block.sync
                def _(e):
                    replay(e, P.q['sp'])
    cfg['_ninstr'] = {e: len(P.q[e]) for e in ENGS}
    return nc


class _Stop(Exception):
    pass


def lam_init(l):
    import math
    return 0.8 - 0.6 * math.exp(-0.3 * l)


def gen_program(nc, P, A, PS, PST, cfg, I, CI, O, Wb, consts, dmin, SCR, SS):
    NL, NSEQ, S = cfg['NL'], cfg['NSEQ'], cfg['S']
    T = 512
    NT = S // T
    NKT = S // 128

    xs_t = A.get([4, D], F32)
    hT = A.get([8, T], BF16)
    hpre = A.get([4, D], BF16)
    mT = hpre.rearrange("p b d -> p (b d)").rearrange("p (c t) -> p c t", c=8)
    ssq = A.get([8], F32)
    rstd = A.get([8], F32)
    sm_t = Trk()
    XS, HT, HP = Trk(), Trk(), Trk()
    stage_f = xs_t.rearrange("p b d -> p (b d)")
    stg_t = XS

    C = {}
    CT = Trk()

    def shaped(ap2d, fs):
        if len(fs) > 1:
            names = ' '.join('a%d' % i for i in range(len(fs)))
            kw = {'a%d' % i: int(fs[i]) for i in range(1, len(fs))}
            return ap2d.rearrange('p (%s) -> p %s' % (names, names), **kw)
        return ap2d
    for k, v in consts.items():
        fs = list(v.shape[1:])
        np_ = v.shape[0]
        n = int(np.prod(fs))
        if k in CONST_BF:
            dst = A.get(fs, BF16)
            st = shaped(stage_f[:, 0:n], fs)
            P.dma('sp', st[0:np_], CI[k], W=[stg_t])
            P.cp('dve', dst[0:np_], st[0:np_], R=[stg_t], W=[CT])
        else:
            dst = A.get(fs, F32)
            P.dma('sp', dst[0:np_], CI[k], W=[CT])
        C[k] = dst
    identb = C['ident']
    identf = A.get([128], F32)
    P.dma('sp', identf, CI['ident'], W=[CT])

    gcol = A.get([NL, 4, 8], F32)
    bgcol = A.get([NL, 4, 8], F32)
    for l in range(NL):
        for n_ in range(4):
            P.dma('sp', gcol[:, l, n_, :], I['g_norm'][l, n_].rearrange("(c p) -> p c", p=128), WA=[CT])
            P.dma('sp', bgcol[:, l, n_, :], I['b_gate'][l, n_].rearrange("(c p) -> p c", p=128), WA=[CT])
    cwcol = A.get([NL, 3, 2], F32)
    for l in range(NL):
        for k_ in range(3):
            P.dma('sp', cwcol[:, l, k_, :], I['conv_w'][l, k_].rearrange("(c p) -> p c", p=128), WA=[CT])
    pscol = A.get([NL, 2], F32)
    for l in range(NL):
        P.dma('sp', pscol[:, l, :], I['pool_scale'][l].rearrange("(c p) -> p c", p=128), WA=[CT])
    gsub = A.get([NL], F32)
    for hh in range(2):
        P.dma('sp', gsub[64 * hh:64 * hh + 64, :], I['diff_subln'].rearrange("l e -> e l"), WA=[CT])
    for l in range(NL):
        P.ts('dve', gsub[:, l:l + 1], gsub[:, l:l + 1], 1.0 - lam_init(l), None, ALU.mult, R=[CT], W=[CT])
    bfg = A.get([NL, 4], F32)
    P.dma('sp', bfg.rearrange("p l h -> p (l h)"), I['b_forget'].rearrange("l h -> (l h)").partition_broadcast(128), W=[CT])
    epsc = A.get([1], F32)
    P.memset('dve', epsc, 1e-6, W=[CT])
    onec = A.get([1], F32)
    P.memset('dve', onec, 1.0, W=[CT])
    lqk = A.get([NL, 4, 32], F32)
    P.dma('sp', lqk.rearrange("p l a d -> p (l a d)"), I['lambda_qk'].rearrange("l a d -> (l a d)").partition_broadcast(128), W=[CT])
    lprod = A.get([NL, 2, 32], F32)
    P.tt('dve', lprod, lqk[:, :, 0:4:2, :], lqk[:, :, 1:4:2, :], ALU.mult, R=[CT], W=[CT])
    lsum = A.get([NL, 2], F32)
    P.op('dve', lambda e: e.reduce_sum(lsum, lprod, mybir.AxisListType.X), R=[CT], W=[CT])
    lexp = A.get([NL, 2], F32)
    P.act(lexp, lsum, AF.Exp, R=[CT], W=[CT])
    neglam = A.get([NL], F32)
    P.tt('dve', neglam, lexp[:, :, 1], lexp[:, :, 0], ALU.subtract, R=[CT], W=[CT])
    for l in range(NL):
        P.ts('dve', neglam[:, l:l + 1], neglam[:, l:l + 1], -lam_init(l), None, ALU.add, R=[CT], W=[CT])
    poolW = A.get([NL, 2, 128], BF16)
    pw_st = stage_f[:, 0:NL * 256].rearrange("p (l c e) -> p l c e", l=NL, c=2)
    P.memset('dve', pw_st, 0.0, W=[stg_t])
    for l in range(NL):
        for g in range(4):
            ch, hf = g // 2, g % 2
            P.dma('sp', pw_st[64 * hf:64 * hf + 64, l, ch, 64 * hf:64 * hf + 64], I['pool_w'][l, g], WA=[stg_t])
    P.cp('dve', poolW, pw_st, R=[stg_t], W=[CT])
    gbc = A.get([2, D], F32)
    gbc_t = Trk()

    WT = [Trk() for _ in range(NL)]

    def convert_layer(l):
        kw = dict(WA=[WT[l]], pool='cv')
        P.dma('pool', Wb['win'][l], I['w_in'][l].rearrange("(c p) n -> p c n", p=128), **kw)
        for i in range(4):
            P.dma('pool', Wb['wg'][l, i], I['w_gate'][l, i].rearrange("(c p) n -> p c n", p=128), **kw)
            P.dma('pool', Wb['wbr'][l, :, i], I['w_branch'][l, i].rearrange("(c p) n -> p c n", p=128), **kw)
        P.dma('pool', Wb['wo'][l], I['w_out'][l].rearrange("(c p) n -> p c n", p=128), **kw)
        wfi = I['w_ffn_in'][l].rearrange("(c p) n -> p c n", p=128)
        for j in range(6):
            w = 512 if j < 5 else 256
            for gu in range(2):
                P.dma('pool', Wb['wfi'][l, j, :, :, gu, 0:w], wfi[:, :, gu * DFF + 512 * j: gu * DFF + 512 * j + w], **kw)
        wfo = I['w_ffn_out'][l].rearrange("(c p) n -> p c n", p=128)
        for j in range(3):
            n = 8 if j < 2 else 6
            P.dma('pool', Wb['wfo'][l, j, :, 0:n, :], wfo[:, 8 * j:8 * j + n, :], **kw)

    convert_layer(0)
    P.barrier()

    scr_t = {k: Trk() for k in SCR}
    xres_t = Trk()
    psrot = {'i': 0}

    def chk(n):
        if cfg.get('stop', 99) == n:
            raise _Stop()

    def bank(lst):
        i = lst[psrot['i'] % len(lst)]
        psrot['i'] += 1
        return PS[i], PST[i]

    def norm_to_hT(l, n, nb=4, bp=128):
        P.memset('dve', ssq[:bp, 0:nb], 0.0, W=[sm_t])
        for b in range(nb):
            P.act(hpre[:bp, b, :], xs_t[:bp, b, :], AF.Square, R=[XS], W=[HP, sm_t], accum_out=ssq[:bp, b:b + 1])
        P.act(rstd[:bp, 0:nb], ssq[:bp, 0:nb], AF.Sqrt, R=[sm_t, CT], W=[sm_t], bias=epsc[:bp], scale=1.0 / D)
        P.recip(rstd[:bp, 0:nb], rstd[:bp, 0:nb], R=[sm_t], W=[sm_t])
        for b in range(nb):
            P.act(hpre[:bp, b, :], xs_t[:bp, b, :], AF.Identity, R=[XS, sm_t], W=[HP], scale=rstd[:bp, b:b + 1])
        for c in range(8):
            ps, pt = bank([0, 1])
            psb = ps.bitcast(BF16)
            for b in range(nb):
                P.tr(psb[:, b * bp:(b + 1) * bp], hpre[:bp, b, c * 128:(c + 1) * 128], identb[:bp, :bp],
                     R=[HP, CT], W=[pt], tick=(b == nb - 1))
            P.ts('dve', hT[:, c, 0:nb * bp], psb[:, 0:nb * bp], gcol[:, l, n, c:c + 1], None, ALU.mult,
                 R=[pt, CT], W=[HT])

    def pass_a(l, s):
        A.push()
        win = A.get([8, NIN], BF16)
        WIN = Trk()
        P.dma('sp', win, Wb['win'][l], R=[WT[l]], W=[WIN])
        stage = A.get([3, 512], F32)
        STG = [Trk() for _ in range(3)]
        vst = A.get([16, 3, 64], BF16)
        vst6 = vst.rearrange("p (k b a) s e -> p k b a s e", k=4, b=2)
        VST = Trk()
        P.memset('pool', vst[:, :, 1, :], 1.0, W=[VST])
        kst = A.get([5, T], BF16)
        KST = Trk()
        qst = A.get([5, T], BF16)
        QST = Trk()
        zext = A.get([2, T + 2], F32)
        ZX = Trk()
        P.memset('pool', zext[:, :, 0:2], 0.0, W=[ZX])
        axs = A.get([T], F32)
        AXS = Trk()
        y1 = A.get([T], F32)
        y2 = A.get([T], F32)
        YT = Trk()
        uext = A.get([2, T + 15], F32)
        UX = Trk()
        P.memset('pool', uext[:, :, 0:15], 0.0, W=[UX])
        sw = A.get([4, T + 15], F32)
        SW = Trk()
        dTt = A.get([2, T], BF16)
        DT = Trk()
        tmp16 = A.get([16], F32)
        oast = A.get([2, T], BF16)
        OAS = Trk()
        odst = A.get([2, T], BF16)
        ODS = Trk()
        lf = A.get([4, 4], F32)
        lfb = A.get([4, 4], BF16)
        lft = A.get([4, 4], F32)
        nFt = A.get([4, 4], F32)
        tot = A.get([4], F32)
        LF = Trk()
        TOT = Trk()
        P.memset('dve', tot, 0.0, W=[TOT])
        cqs = A.get([T], BF16)
        CQS = Trk()
        sti = {'i': 0}

        for t in range(NT if cfg.get('stop', 99) == 99 else 1):
          try:
            t0 = t * T
            if t > 0:
                P.cp('pool', zext[:, :, 0:2], zext[:, :, T:T + 2], R=[ZX], W=[ZX])
                P.cp('pool', uext[:, :, 0:15], uext[:, :, T:T + 15], R=[UX], W=[UX])
            src = I['xp'][s, t0:t0 + T, :] if l == 0 else SCR['xres'][s, t0:t0 + T, :]
            P.dma('sp', xs_t, src.rearrange("(b p) d -> p b d", p=128), R=[xres_t], W=[XS])
            chk(0)
            norm_to_hT(l, 0)
            chk(1)
            P.dma('sp', SCR['hT'][:, :, t0:t0 + T], hT, R=[HT], WA=[scr_t['hT']])
            pl, plt = PS[6], PST[6]
            for b in range(4):
                for gi, (c0, okn, ovn) in enumerate(((C_DK, 'p_dk', 'p_dv'), (C_FK, 'p_fk', 'p_fv'))):
                    ps, pt = bank([2, 3, 4, 5])
                    for c in range(8):
                        P.mm(ps, hT[:, c, b * 128:(b + 1) * 128], win[:, c, c0:c0 + 512], start=(c == 0), stop=(c == 7),
                             R=[HT, WIN], W=[pt])
                    si = sti['i'] % 3
                    sti['i'] += 1
                    P.cp('act', stage[:, si, :], ps, R=[pt], W=[STG[si]])
                    P.cp('dve', vst6[:, b, gi, :, 0:3:2, :], ps[:, 256:512].rearrange("p (a s e) -> p a s e", a=2, s=2),
                         R=[pt], W=[VST])
                    P.dma('pool', O[okn][l, s, t0 + b * 128:t0 + (b + 1) * 128, :], stage[:, si, 0:256], R=[STG[si]])
                    P.dma('pool', O[ovn][l, s, t0 + b * 128:t0 + (b + 1) * 128, :], stage[:, si, 256:512], R=[STG[si]])
                for c in range(8):
                    P.mm(pl[:, b * 4:(b + 1) * 4], hT[:, c, b * 128:(b + 1) * 128], win[:, c, C_FF:C_FF + 4],
                         start=(c == 0), stop=(c == 7), R=[HT, WIN], W=[plt])
            P.dma('sp', SCR['v'][:, 4 * t:4 * t + 4].rearrange("p k b a s e -> p (k b a) s e"), vst, R=[VST], WA=[scr_t['v']])
            chk(2)
            P.tt('dve', lft, pl[:, 0:16].rearrange("p (b h) -> p b h", h=4),
                 bfg[:, l:l + 1, :].to_broadcast([128, 4, 4]), ALU.add, R=[plt, CT], W=[LF])
            P.act(lft, lft, AF.Exp, R=[LF], W=[LF], scale=-1.0)
            P.act(lft, lft, AF.Ln, R=[LF, CT], W=[LF], bias=onec, scale=1.0)
            P.ts('dve', lf, lft, -1.0, None, ALU.mult, R=[LF], W=[LF])
            P.dma('pool', O['p_lf'][l, s, t0:t0 + T, :].rearrange("(b p) h -> p b h", p=128), lf, R=[LF])
            chk(3)
            pf, pft = PS[7], PST[7]
            for b in range(4):
                P.mm(pf[:, b * 4:(b + 1) * 4], C['tri'], lf[:, b, :], start=True, stop=(b == 0), R=[LF, CT], W=[pft])
                for b2 in range(b):
                    P.mm(pf[:, b * 4:(b + 1) * 4], C['ones'], lf[:, b2, :], start=False, stop=(b2 == b - 1), R=[LF, CT], W=[pft])
            for b in range(4):
                P.mm(pf[:, 16:20], C['ones'], lf[:, b, :], start=(b == 0), stop=(b == 3), R=[LF, CT], W=[pft])
            P.stt('dve', nFt, pf[:, 0:16].rearrange("p (b h) -> p b h", h=4), -1.0,
                  tot.unsqueeze(1).to_broadcast([128, 4, 4]), ALU.mult, ALU.subtract, R=[pft, TOT], W=[LF])
            P.dma('sp', SCR['nF'][:, 4 * t:4 * t + 4, :], nFt, R=[LF], WA=[scr_t['nF']])
            P.cp('dve', lfb, lf, R=[LF], W=[LF])
            P.tt('dve', lfb[0:1, 0, :], lf[0:1, 0, :], tot[0:1, :], ALU.add, R=[LF, TOT], W=[LF])
            P.tt('dve', tot, tot, pf[:, 16:20], ALU.add, R=[pft, TOT], W=[TOT])
            pc, pct = bank([2, 3, 4, 5])
            for b in range(4):
                P.mm(pc[0:4, :], lfb[:, b, :], C['twide'][:, 384 - 128 * b:384 - 128 * b + T], start=(b == 0), stop=(b == 3),
                     R=[LF, CT], W=[pct])
            P.cp('act', cqs[0:4, :], pc[0:4, :], R=[pct], W=[CQS])
            P.dma('sp', SCR['cq'][:, t0:t0 + T], cqs[0:4, :], R=[CQS], WA=[scr_t['cq']])

            chk(4)
            def fm(col0, m):
                ps, pt = bank([2, 3, 4, 5])
                for c in range(8):
                    P.mm(ps[0:m, :], win[:, c, col0:col0 + m], hT[:, c, :], start=(c == 0), stop=(c == 7), R=[HT, WIN], W=[pt])
                return ps, pt
            for j, (o, m) in enumerate(DCH):
                ps, pt = fm(C_DQ + o, m)
                P.ts('dve', qst[0:m, j, :], ps[0:m, :], 32.0 ** -0.5, None, ALU.mult, R=[pt], W=[QST])
                ps, pt = fm(C_DK + o, m)
                P.cp('act', kst[0:m, j, :], ps[0:m, :], R=[pt], W=[KST])
            for j in range(2):
                ps, pt = fm(C_FQ + 128 * j, 128)
                P.ts('dve', qst[:, 3 + j, :], ps, 0.125, None, ALU.mult, R=[pt], W=[QST])
                ps, pt = fm(C_FK + 128 * j, 128)
                P.cp('act', kst[:, 3 + j, :], ps, R=[pt], W=[KST])
            P.dma('sp', SCR['qT'][:, :, t0:t0 + T], qst, R=[QST], WA=[scr_t['qT']])
            P.dma('sp', SCR['kT'][:, :, t0:t0 + T], kst, R=[KST], WA=[scr_t['kT']])
            chk(5)
            for ch in range(2):
                ps, pt = fm(C_AX + 128 * ch, 128)
                P.cp('act', axs, ps, R=[pt], W=[AXS])
                ps, pt = fm(C_AC + 128 * ch, 128)
                P.tt('dve', zext[:, ch, 2:2 + T], ps, axs, ALU.mult, R=[pt, AXS], W=[ZX])
                P.ts('pool', y1, zext[:, ch, 0:T], cwcol[:, l, 0, ch:ch + 1], None, ALU.mult, R=[ZX, CT], W=[YT])
                P.stt('dve', y2, zext[:, ch, 1:1 + T], cwcol[:, l, 1, ch:ch + 1], y1, ALU.mult, ALU.add, R=[ZX, YT, CT], W=[YT])
                P.stt('dve', y1, zext[:, ch, 2:2 + T], cwcol[:, l, 2, ch:ch + 1], y2, ALU.mult, ALU.add, R=[ZX, YT, CT], W=[YT])
                ps, pt = fm(C_AB + 128 * ch, 128)
                P.tt('dve', oast[:, ch, :], ps, y1, ALU.mult, R=[pt, YT], W=[OAS])
            P.dma('sp', SCR['obr'][0, :, :, t0:t0 + T], oast, R=[OAS], WA=[scr_t['obr']])
            if t == NT - 1:
                for ch in range(2):
                    P.dma('pool', O['p_conv'][l, s, :, ch * 128:(ch + 1) * 128].rearrange("t c -> c t"), zext[:, ch, T:T + 2], R=[ZX])
            chk(6)
            E = T + 15
            for ch in range(2):
                ps, pt = fm(C_PU + 128 * ch, 128)
                P.cp('act', uext[:, ch, 15:15 + T], ps, R=[pt], W=[UX])
                P.tt('pool', sw[:, 0, 1:E], uext[:, ch, 1:E], uext[:, ch, 0:E - 1], ALU.add, R=[UX], W=[SW])
                P.tt('pool', sw[:, 1, 3:E], sw[:, 0, 3:E], sw[:, 0, 1:E - 2], ALU.add, R=[SW], W=[SW])
                if ch == 1:
                    P.tt('pool', sw[:, 2, 7:E], sw[:, 1, 7:E], sw[:, 1, 3:E - 4], ALU.add, R=[SW], W=[SW])
                    P.tt('pool', sw[:, 3, 15:E], sw[:, 2, 15:E], sw[:, 2, 7:E - 8], ALU.add, R=[SW], W=[SW])
                for hf in range(2):
                    g = ch * 2 + hf
                    pr = slice(64 * hf, 64 * hf + 64)
                    P.stt('dve', dTt[pr, ch, :], sw[pr, g, 15:15 + T], 1.0 / POOLW[g], uext[pr, ch, 15:15 + T],
                          ALU.mult, ALU.subtract, R=[SW, UX], W=[DT])
                    if t == 0:
                        P.tt('dve', tmp16[pr, :], sw[pr, g, 15:31], C['invcnt'][pr, ch, :], ALU.mult, R=[SW, CT, DT], W=[DT])
                        P.tt('dve', dTt[pr, ch, 0:16], tmp16[pr, :], uext[pr, ch, 15:31], ALU.subtract, R=[DT, UX], W=[DT])
                ps, pt = bank([2, 3, 4, 5])
                P.mm(ps, poolW[:, l, ch, :], dTt[:, ch, :], R=[DT, CT], W=[pt])
                P.act(odst[:, ch, :], ps, AF.Identity, R=[pt, CT], W=[ODS], scale=pscol[:, l, ch:ch + 1])
            P.dma('sp', SCR['obr'][3, :, :, t0:t0 + T], odst, R=[ODS], WA=[scr_t['obr']])
            if t == NT - 1:
                for ch in range(2):
                    P.dma('pool', O['p_pool'][l, s, :, ch * 128:(ch + 1) * 128].rearrange("t c -> c t"), uext[:, ch, T:T + 15], R=[UX])
          except _Stop:
            pass
        P.barrier()
        A.pop()

    def pass_b(l, s):
        A.push()
        kT = A.get([5, S], BF16)
        KT = [Trk() for _ in range(5)]
        for j in range(5):
            P.dma('sp', kT[:, j, :], SCR['kT'][:, j, :], W=[KT[j]])
        vv = A.get([NKT, 4, 192], BF16)
        VV = [Trk() for _ in range(4)]
        vsrc = SCR['v'].rearrange("p k b a s e -> p k (b a) (s e)")
        for bp_ in range(4):
            P.dma('sp', vv[:, :, bp_, :], vsrc[:, :, bp_, :], W=[VV[bp_]])
        nF = A.get([NKT, 4], F32)
        NF = Trk()
        P.dma('sp', nF, SCR['nF'], W=[NF])
        cq = A.get([S], BF16)
        CQ = Trk()
        P.memset('pool', cq, 0.0, W=[CQ])
        P.dma('sp', cq[0:4, :], SCR['cq'], WA=[CQ])
        qT = A.get([2, 5, T], BF16)
        QT = [Trk(), Trk()]
        pT = A.get([3, T], BF16)
        PT = [Trk() for _ in range(3)]
        rb = A.get([2, T], F32)
        RB = [Trk(), Trk()]
        o1 = A.get([T], F32)
        t2 = A.get([T], F32)
        OT = Trk()
        odf = A.get([2, T], F32)
        ODF = [Trk(), Trk()]
        sq = A.get([T], F32)
        SQ = Trk()
        obst = A.get([2, 2, T], BF16)
        OBS = [Trk(), Trk()]
        cnt = {'a': 0, 'i': 0}

        def emit_s(it):
            (qi, t, br, h, comp, kt, nkt, ai) = it
            si = it_idx[id(it)] % 3
            ps, pt = PS[si], PST[si]
            diag = kt >= 4 * t
            j = kt - 4 * t
            hh = h % 2
            if br == 0:
                hc = 2 * h + comp
                cj, r = hc // 3, hc % 3
                P.mm(ps, kT[32 * r:32 * r + 32, cj, kt * 128:(kt + 1) * 128], qT[32 * r:32 * r + 32, qi, cj, :],
                     start=True, stop=not diag, R=[KT[cj], QT[qi]], W=[pt])
                if diag:
                    P.mm(ps, identb, C['mwd'][:, h, 384 - 128 * j:384 - 128 * j + T], start=False, stop=True, R=[CT], W=[pt])
            else:
                cj = 3 + h // 2
                P.mm(ps, kT[64 * hh:64 * hh + 64, cj, kt * 128:(kt + 1) * 128], qT[64 * hh:64 * hh + 64, qi, cj, :],
                     start=True, stop=False, R=[KT[cj], QT[qi]], W=[pt])
                P.mm(ps, C['sel4'][:, h, :], cq[:, t * T:(t + 1) * T], start=False, stop=not diag, R=[CT, CQ], W=[pt])
                if diag:
                    P.mm(ps, identb, C['mwf'][:, 384 - 128 * j:384 - 128 * j + T], start=False, stop=True, R=[CT], W=[pt])
            pi = si
            if br == 0:
                d0 = 128 * kt - 512 * t - 256
                if h == 0:
                    for half in range(2):
                        dd = d0 + 128 - 256 * half
                        di = (dd - dmin) // 128
                        P.act(pT[:, pi, 256 * half:256 * half + 256], ps[:, 256 * half:256 * half + 256], AF.Exp,
                              R=[pt, CT], W=[PT[pi]], bias=C['alibi'][:, h, di:di + 1], scale=1.0)
                else:
                    di = (d0 - dmin) // 128
                    P.act(pT[:, pi, :], ps, AF.Exp, R=[pt, CT], W=[PT[pi]], bias=C['alibi'][:, h, di:di + 1], scale=1.0)
            else:
                P.act(pT[:, pi, :], ps, AF.Exp, R=[pt, NF], W=[PT[pi]], bias=nF[:, kt, h:h + 1], scale=1.0)

        def emit_pv(it):
            (qi, t, br, h, comp, kt, nkt, ai) = it
            pi = it_idx[id(it)] % 3
            pair, hh = h // 2, h % 2
            P.mm(PS[ai], vv[:, kt, br * 2 + pair, 64 * hh:64 * hh + 128], pT[:, pi, :],
                 start=(kt == 0), stop=(kt == nkt - 1), R=[VV[br * 2 + pair], PT[pi]], W=[PST[ai]])

        def finish(l, br, h, accs):
            pair, hh = h // 2, h % 2
            orng = slice(64 * hh, 64 * hh + 64)
            drng = slice(64 * (1 - hh), 64 * (1 - hh) + 64)
            if br == 0:
                a0, a1 = accs
                P.recip(rb[orng, 0, :], PS[a0][drng, :], R=[PST[a0]], W=[RB[0]])
                P.tt('dve', o1[orng, :], PS[a0][orng, :], rb[orng, 0, :], ALU.mult, R=[PST[a0], RB[0]], W=[OT])
                P.recip(rb[orng, 1, :], PS[a1][drng, :], R=[PST[a1]], W=[RB[1]])
                P.tt('dve', t2[orng, :], PS[a1][orng, :], rb[orng, 1, :], ALU.mult, R=[PST[a1], RB[1], OT], W=[OT])
                P.stt('dve', odf[orng, pair, :], t2[orng, :], neglam[orng, l:l + 1], o1[orng, :], ALU.mult, ALU.add,
                      R=[OT, CT], W=[ODF[pair]])
                if hh == 1:
                    P.act(sq, odf[:, pair, :], AF.Square, R=[ODF[pair]], W=[SQ])
                    pm, pmt = PS[7], PST[7]
                    P.mm(pm, C['blk64'], sq, R=[SQ, CT], W=[pmt])
                    P.act(sq, pm, AF.Sqrt, R=[pmt, CT], W=[SQ], bias=epsc, scale=1.0)
                    P.recip(sq, sq, R=[SQ], W=[SQ])
                    P.stt('dve', obst[:, 0, pair, :], odf[:, pair, :], gsub[:, l:l + 1], sq, ALU.mult, ALU.mult,
                          R=[ODF[pair], SQ, CT], W=[OBS[0]])
            else:
                a0 = accs[0]
                P.recip(rb[orng, 0, :], PS[a0][drng, :], R=[PST[a0]], W=[RB[0]])
                P.tt('dve', obst[orng, 1, pair, :], PS[a0][orng, :], rb[orng, 0, :], ALU.mult, R=[PST[a0], RB[0]], W=[OBS[1]])

        it_idx = {}
        for t in range(NT):
            t0 = t * T
            qi = t % 2
            P.dma('sp', qT[:, qi], SCR['qT'][:, :, t0:t0 + T], W=[QT[qi]])
            items = []
            fin = {}
            nkt = 4 * t + 4
            for br in range(2):
                for h in range(4):
                    accs = []
                    for comp in range(2 if br == 0 else 1):
                        ai = 3 + cnt['a'] % 4
                        cnt['a'] += 1
                        accs.append(ai)
                        for kt in range(nkt):
                            items.append((qi, t, br, h, comp, kt, nkt, ai))
                    fin[len(items) - 1] = (br, h, accs)
            for it in items:
                it_idx[id(it)] = cnt['i']
                cnt['i'] += 1
            LAG = 2
            for i in range(len(items) + LAG):
                if i < len(items):
                    emit_s(items[i])
                if i >= LAG:
                    emit_pv(items[i - LAG])
                    if (i - LAG) in fin:
                        br, h, accs = fin[i - LAG]
                        finish(l, br, h, accs)
                        if h == 3:
                            P.dma('sp', SCR['obr'][1 + br, :, :, t0:t0 + T], obst[:, br], R=[OBS[br]], WA=[scr_t['obr']])
            it_idx.clear()
        P.barrier()
        A.pop()

    def pass_c(l, tiles):
        A.push()
        NSLOT = 4
        ring = A.get([NSLOT, 8 * D], BF16)
        RG = [Trk() for _ in range(NSLOT)]
        pieces = [('g', i) for i in range(4)] + [('o', 0)] + [('fi', j) for j in range(6)] + [('fo', j) for j in range(3)]
        NP = len(pieces)

        def piece_src(p):
            k, j = p
            if k == 'g':
                return Wb['wg'][l, j].rearrange("p c n -> p (c n)")
            if k == 'o':
                return Wb['wo'][l].rearrange("p c n -> p (c n)")
            if k == 'fi':
                return Wb['wfi'][l, j].rearrange("p c g n -> p (c g n)")
            return Wb['wfo'][l, j].rearrange("p c n -> p (c n)")
        st = {'issued': 0}
        total = len(tiles) * NP

        def prefetch(upto):
            while st['issued'] < min(upto, total):
                i = st['issued']
                slot = i % NSLOT
                P.dma('sp', ring[:, slot, :], piece_src(pieces[i % NP]), R=[WT[l]], W=[RG[slot]])
                st['issued'] += 1

        def getp(gi, first_needed=None):
            prefetch((gi if first_needed is None else first_needed) + NSLOT)
            slot = gi % NSLOT
            return ring[:, slot, :], RG[slot]

        wbrv = A.get([4, 2, D], BF16)
        WBR = Trk()
        P.dma('sp', wbrv, Wb['wbr'][l], R=[WT[l]], W=[WBR])
        P.dma('sp', gbc[:, 0, :], I['g_norm'][l, 1].partition_broadcast(128), WA=[gbc_t])
        P.dma('sp', gbc[:, 1, :], I['g_norm'][l, 3].partition_broadcast(128), WA=[gbc_t])
        obr = A.get([4, 2, T], BF16)
        OBR = [Trk() for _ in range(4)]
        accm = A.get([8, T], F32)
        ACC = Trk()
        MT = HP
        gate = A.get([2, T], F32)
        GT = [Trk(), Trk()]
        tmpf = A.get([2, T], F32)
        TM = [Trk(), Trk()]
        aT = A.get([NFC, T], BF16)
        AT = Trk()
        ss2 = A.get([8], F32)
        rs2 = A.get([4], F32)
        S2 = Trk()
        cc = {'g': 0, 't': 0}
        CB = [2, 3, 4, 5, 6, 7]

        def out_norm_residual(lhs, lhs_t, nk, getw, gi_norm, nb):
            for b in range(nb):
                P.memset('dve', ss2[:, 2 * b:2 * b + 2], 0.0, W=[S2])
                halves = []
                for half in range(2):
                    ps, pt = bank(CB)
                    for k in range(nk):
                        w, wt = getw(k)
                        P.mm(ps, lhs[:, k, b * 128:(b + 1) * 128], w[:, half * 512:(half + 1) * 512], start=(k == 0), stop=(k == nk - 1),
                             R=[lhs_t, wt], W=[pt])
                    ti = cc['t'] % 2
                    cc['t'] += 1
                    P.act(tmpf[:, ti, :], ps, AF.Square, R=[pt], W=[TM[ti], S2], accum_out=ss2[:, 2 * b + half:2 * b + half + 1])
                    halves.append((ps, pt))
                P.tt('dve', rs2[:, b:b + 1], ss2[:, 2 * b:2 * b + 1], ss2[:, 2 * b + 1:2 * b + 2], ALU.add, R=[S2], W=[S2])
                P.act(rs2[:, b:b + 1], rs2[:, b:b + 1], AF.Sqrt, R=[S2, CT], W=[S2], bias=epsc, scale=1.0 / D)
                P.recip(rs2[:, b:b + 1], rs2[:, b:b + 1], R=[S2], W=[S2])
                for half, (ps, pt) in enumerate(halves):
                    ti = cc['t'] % 2
                    cc['t'] += 1
                    P.stt('dve', tmpf[:, ti, :], ps, rs2[:, b:b + 1], gbc[:, gi_norm, half * 512:(half + 1) * 512], ALU.mult, ALU.mult,
                          R=[pt, S2, gbc_t], W=[TM[ti]])
                    P.tt('pool', xs_t[:, b, half * 512:(half + 1) * 512], xs_t[:, b, half * 512:(half + 1) * 512], tmpf[:, ti, :], ALU.add,
                         R=[TM[ti], XS], W=[XS])

        for ti_, tl in enumerate(tiles):
            nb = tl['nb']
            Tn = nb * 128
            fs = slice(0, Tn)
            g0 = ti_ * NP
            prefetch(g0 + NSLOT)
            P.dma('sp', xs_t[:, 0:nb, :], tl['src'], R=[xres_t], W=[XS])
            P.dma('sp', hT[:, :, fs], tl['hT'], R=[scr_t['hT']], W=[HT])
            for i in range(4):
                P.dma('sp', obr[:, i, :, fs], tl['obr'][i], R=[scr_t['obr']], W=[OBR[i]])
            for i in range(4):
                wg, wgt = getp(g0 + i)
                wgv = wg.rearrange("p (c n) -> p c n", c=8)
                for m in range(8):
                    ps, pt = bank(CB)
                    for c in range(8):
                        P.mm(ps[:, fs], wgv[:, c, m * 128:(m + 1) * 128], hT[:, c, fs], start=(c == 0), stop=(c == 7), R=[wgt, HT], W=[pt])
                    gi = cc['g'] % 2
                    cc['g'] += 1
                    P.act(gate[:, gi, fs], ps[:, fs], AF.Sigmoid, R=[pt, CT], W=[GT[gi]], bias=bgcol[:, l, i, m:m + 1], scale=1.0)
                    ps2, pt2 = bank(CB)
                    for c in range(2):
                        P.mm(ps2[:, fs], wbrv[:, i, c, m * 128:(m + 1) * 128], obr[:, i, c, fs], start=(c == 0), stop=(c == 1), R=[WBR, OBR[i]], W=[pt2])
                    if i == 0:
                        P.tt('dve', accm[:, m, fs], ps2[:, fs], gate[:, gi, fs], ALU.mult, R=[pt2, GT[gi]], W=[ACC])
                    else:
                        ti = cc['t'] % 2
                        cc['t'] += 1
                        P.tt('dve', tmpf[:, ti, fs], ps2[:, fs], gate[:, gi, fs], ALU.mult, R=[pt2, GT[gi]], W=[TM[ti]])
                        if i < 3:
                            P.tt('pool', accm[:, m, fs], accm[:, m, fs], tmpf[:, ti, fs], ALU.add, R=[TM[ti], ACC], W=[ACC])
                        else:
                            P.tt('pool', mT[:, m, fs], accm[:, m, fs], tmpf[:, ti, fs], ALU.add, R=[TM[ti], ACC], W=[MT])
            wo, wot = getp(g0 + 4)
            wov = wo.rearrange("p (c n) -> p c n", c=8)
            out_norm_residual(mT, MT, 8, lambda k: (wov[:, k, :], wot), 0, nb)
            norm_to_hT(l, 2, nb=nb)
            for j in range(6):
                wf, wft = getp(g0 + 5 + j)
                wfv = wf.rearrange("p (c g n) -> p c g n", c=8, g=2)
                for jj in range(4 if j < 5 else 2):
                    fc = 4 * j + jj
                    psg, ptg = bank(CB)
                    for c in range(8):
                        P.mm(psg[:, fs], wfv[:, c, 0, jj * 128:(jj + 1) * 128], hT[:, c, fs], start=(c == 0), stop=(c == 7), R=[wft, HT], W=[ptg])
                    psu, ptu = bank(CB)
                    for c in range(8):
                        P.mm(psu[:, fs], wfv[:, c, 1, jj * 128:(jj + 1) * 128], hT[:, c, fs], start=(c == 0), stop=(c == 7), R=[wft, HT], W=[ptu])
                    gi = cc['g'] % 2
                    cc['g'] += 1
                    P.act(gate[:, gi, fs], psg[:, fs], AF.Silu, R=[ptg], W=[GT[gi]])
                    P.tt('dve', aT[:, fc, fs], psu[:, fs], gate[:, gi, fs], ALU.mult, R=[ptu, GT[gi]], W=[AT])
            wfo = [getp(g0 + 11 + j, g0 + 11) for j in range(3)]

            def getfo(k):
                w, wt = wfo[k // 8]
                return w.rearrange("p (c n) -> p c n", c=8)[:, k % 8, :], wt
            out_norm_residual(aT, AT, NFC, getfo, 1, nb)
            P.dma('pool', tl['dst'], xs_t[:, 0:nb, :], R=[XS], WA=[xres_t])
        P.barrier()
        A.pop()

    def prompt_tiles(l, s):
        last = (l == NL - 1)
        tl = []
        for t in range(NT):
            t0 = t * T
            src = I['xp'][s, t0:t0 + T, :] if l == 0 else SCR['xres'][s, t0:t0 + T, :]
            dst = O['yp'][s, t0:t0 + T, :] if last else SCR['xres'][s, t0:t0 + T, :]
            tl.append(dict(src=src.rearrange("(b p) d -> p b d", p=128), dst=dst.rearrange("(b p) d -> p b d", p=128),
                           hT=SCR['hT'][:, :, t0:t0 + T], obr=[SCR['obr'][i, :, :, t0:t0 + T] for i in range(4)], nb=4))
        return tl

    NS, PL = cfg.get('NS', 0), cfg.get('PL', 0)
    KT = PL // 128

    def pass_a_s(l):
        A.push()
        Tn = 128
        win = A.get([8, NIN], BF16)
        WIN = Trk()
        P.dma('sp', win, Wb['win'][l], R=[WT[l]], W=[WIN])
        stage = A.get([2, 512], F32)
        STG = [Trk(), Trk()]
        vst = A.get([4, 3, 64], BF16)
        vst4 = vst.rearrange("p (b a) s e -> p b a s e", b=2)
        VST = Trk()
        P.memset('pool', vst[:, :, 1, :], 1.0, W=[VST])
        kst = A.get([5, Tn], BF16)
        qst = A.get([5, Tn], BF16)
        KST, QST = Trk(), Trk()
        zext = A.get([2, NS, 18], F32)
        ZX = Trk()
        uext = A.get([2, NS, 31], F32)
        UX = Trk()
        for i in range(NS):
            for ch in range(2):
                P.dma('sp', zext[:, ch, i, 0:2], I['sconv'][l, i, :, ch * 128:(ch + 1) * 128].rearrange("t c -> c t"), WA=[ZX])
                P.dma('sp', uext[:, ch, i, 0:15], I['spool'][l, i, :, ch * 128:(ch + 1) * 128].rearrange("t c -> c t"), WA=[UX])
        axs = A.get([Tn], F32)
        AXS = Trk()
        y1 = A.get([NS, 16], F32)
        y2 = A.get([NS, 16], F32)
        YT = Trk()
        sw = A.get([4, NS, 31], F32)
        SW = Trk()
        dTt = A.get([2, Tn], BF16)
        DT = Trk()
        P.memset('dve', dTt, 0.0, W=[DT])
        oast = A.get([2, Tn], BF16)
        OAS = Trk()
        P.memset('dve', oast, 0.0, W=[OAS])
        odst = A.get([2, Tn], BF16)
        ODS = Trk()
        hlf = A.get([NS, KT, 4], F32)
        HLF = Trk()
        for i in range(NS):
            P.dma('sp', hlf[:, i], I['clf'][l, i].rearrange("(k p) h -> p k h", p=128), WA=[HLF])
        pa = A.get([KT, 4], F32)
        pb = A.get([KT, 4], F32)
        PFX = Trk()
        nFs = A.get([NS, KT + 1, 4], F32)
        NFS = Trk()
        P.memset('dve', nFs, 0.0, W=[NFS])
        ftot = A.get([NS, 4], F32)
        FT = Trk()
        lf = A.get([4], F32)
        lft = A.get([4], F32)
        lfb = A.get([4], BF16)
        LF = Trk()
        cqs = A.get([Tn], BF16)
        CQS = Trk()

        def v3(ap2d):
            return ap2d.rearrange("p (i t) -> p i t", t=32)[:, 0:NS, 0:16]

        src = I['xs'] if l == 0 else SS['xres']
        P.dma('sp', xs_t[:, 0, :], src, R=[xres_t], W=[XS])
        norm_to_hT(l, 0, nb=1)
        P.dma('sp', SS['hT'], hT[:, :, 0:Tn], R=[HT], WA=[scr_t['hT']])
        for i in range(NS):
            pf, pft = PS[7], PST[7]
            P.mm(pf[:, 0:KT * 4], C['tri'], hlf[:, i].rearrange("p k h -> p (k h)"), R=[HLF, CT], W=[pft])
            P.mm(pf[:, 128:128 + KT * 4], C['ones'], hlf[:, i].rearrange("p k h -> p (k h)"), R=[HLF, CT], W=[pft])
            P.cp('dve', pa, pf[:, 128:128 + KT * 4].rearrange("p (k h) -> p k h", h=4), R=[pft], W=[PFX])
            cur, oth = pa, pb
            sft = 1
            while sft < KT:
                P.cp('dve', oth, cur, R=[PFX], W=[PFX])
                P.tt('dve', oth[:, sft:, :], cur[:, sft:, :], cur[:, 0:KT - sft, :], ALU.add, R=[PFX], W=[PFX])
                cur, oth = oth, cur
                sft *= 2
            P.cp('dve', ftot[:, i, :], cur[:, KT - 1, :], R=[PFX], W=[FT])
            P.tt('dve', oth, cur, pf[:, 128:128 + KT * 4].rearrange("p (k h) -> p k h", h=4), ALU.subtract, R=[PFX, pft], W=[PFX])
            P.tt('dve', oth, oth, pf[:, 0:KT * 4].rearrange("p (k h) -> p k h", h=4), ALU.add, R=[PFX, pft], W=[PFX])
            P.ts('dve', nFs[:, i, 0:KT, :], oth, -1.0, None, ALU.mult, R=[PFX], W=[NFS])
        pl, plt = PS[6], PST[6]
        for gi, (c0, okn, ovn) in enumerate(((C_DK, 's_dk', 's_dv'), (C_FK, 's_fk', 's_fv'))):
            ps, pt = bank([2, 3, 4, 5])
            for c in range(8):
                P.mm(ps, hT[:, c, 0:128], win[:, c, c0:c0 + 512], start=(c == 0), stop=(c == 7), R=[HT, WIN], W=[pt])
            P.cp('act', stage[:, gi, :], ps, R=[pt], W=[STG[gi]])
            P.cp('dve', vst4[:, gi, :, 0:3:2, :], ps[:, 256:512].rearrange("p (a s e) -> p a s e", a=2, s=2), R=[pt], W=[VST])
            P.dma('pool', O[okn][l], stage[:, gi, 0:256], R=[STG[gi]])
            P.dma('pool', O[ovn][l], stage[:, gi, 256:512], R=[STG[gi]])
        for c in range(8):
            P.mm(pl[:, 0:4], hT[:, c, 0:128], win[:, c, C_FF:C_FF + 4], start=(c == 0), stop=(c == 7), R=[HT, WIN], W=[plt])
        P.dma('sp', SS['v'], vst.rearrange("p a s e -> p a (s e)"), R=[VST], WA=[scr_t['v']])
        P.tt('dve', lft, pl[:, 0:4], bfg[:, l, :], ALU.add, R=[plt, CT], W=[LF])
        P.act(lft, lft, AF.Exp, R=[LF], W=[LF], scale=-1.0)
        P.act(lft, lft, AF.Ln, R=[LF, CT], W=[LF], bias=onec, scale=1.0)
        P.ts('dve', lf, lft, -1.0, None, ALU.mult, R=[LF], W=[LF])
        P.dma('pool', O['s_lf'][l], lf, R=[LF])
        pf, pft = PS[7], PST[7]
        P.mm(pf[:, 0:4], C['bdtri'], lf, R=[LF, CT], W=[pft])
        P.cp('dve', lfb, lf, R=[LF], W=[LF])
        for i in range(NS):
            pr = slice(32 * i, 32 * i + 32)
            P.stt('dve', nFs[pr, i, KT, :], pf[pr, 0:4], -1.0, ftot[pr, i, :], ALU.mult, ALU.subtract, R=[pft, FT], W=[NFS])
            P.tt('dve', lfb[32 * i:32 * i + 1, :], lf[32 * i:32 * i + 1, :], ftot[32 * i:32 * i + 1, i, :], ALU.add, R=[LF, FT], W=[LF])
        P.dma('sp', SS['nF'], nFs, R=[NFS], WA=[scr_t['nF']])
        pc, pct = bank([2, 3, 4, 5])
        P.mm(pc[0:4, 0:128], lfb, C['bdtrib'], R=[LF, CT], W=[pct])
        P.cp('act', cqs[0:4, :], pc[0:4, 0:128], R=[pct], W=[CQS])
        P.dma('sp', SS['cq'], cqs[0:4, :], R=[CQS], WA=[scr_t['cq']])

        def fm(col0, m):
            ps, pt = bank([2, 3, 4, 5])
            for c in range(8):
                P.mm(ps[0:m, 0:Tn], win[:, c, col0:col0 + m], hT[:, c, 0:Tn], start=(c == 0), stop=(c == 7), R=[HT, WIN], W=[pt])
            return ps, pt
        for j, (o, m) in enumerate(DCH):
            ps, pt = fm(C_DQ + o, m)
            P.ts('dve', qst[0:m, j, :], ps[0:m, 0:Tn], 32.0 ** -0.5, None, ALU.mult, R=[pt], W=[QST])
            ps, pt = fm(C_DK + o, m)
            P.cp('act', kst[0:m, j, :], ps[0:m, 0:Tn], R=[pt], W=[KST])
        for j in range(2):
            ps, pt = fm(C_FQ + 128 * j, 128)
            P.ts('dve', qst[:, 3 + j, :], ps[:, 0:Tn], 0.125, None, ALU.mult, R=[pt], W=[QST])
            ps, pt = fm(C_FK + 128 * j, 128)
            P.cp('act', kst[:, 3 + j, :], ps[:, 0:Tn], R=[pt], W=[KST])
        P.dma('sp', SS['qT'], qst, R=[QST], WA=[scr_t['qT']])
        P.dma('sp', SS['kT'], kst, R=[KST], WA=[scr_t['kT']])
        for ch in range(2):
            ps, pt = fm(C_AX + 128 * ch, 128)
            P.cp('act', axs, ps[:, 0:Tn], R=[pt], W=[AXS])
            ps, pt = fm(C_AC + 128 * ch, 128)
            P.tt('dve', zext[:, ch, :, 2:18], v3(ps[:, 0:Tn]), v3(axs), ALU.mult, R=[pt, AXS], W=[ZX])
            P.ts('pool', y1, zext[:, ch, :, 0:16], cwcol[:, l, 0, ch:ch + 1], None, ALU.mult, R=[ZX, CT], W=[YT])
            P.stt('dve', y2, zext[:, ch, :, 1:17], cwcol[:, l, 1, ch:ch + 1], y1, ALU.mult, ALU.add, R=[ZX, YT, CT], W=[YT])
            P.stt('dve', y1, zext[:, ch, :, 2:18], cwcol[:, l, 2, ch:ch + 1], y2, ALU.mult, ALU.add, R=[ZX, YT, CT], W=[YT])
            ps, pt = fm(C_AB + 128 * ch, 128)
            P.tt('dve', v3(oast[:, ch, :]), v3(ps[:, 0:Tn]), y1, ALU.mult, R=[pt, YT], W=[OAS])
            for i in range(NS):
                P.dma('pool', O['s_conv'][l, i, :, ch * 128:(ch + 1) * 128].rearrange("t c -> c t"), zext[:, ch, i, 16:18], R=[ZX])
        P.dma('sp', SS['obr'][0], oast, R=[OAS], WA=[scr_t['obr']])
        for ch in range(2):
            ps, pt = fm(C_PU + 128 * ch, 128)
            P.cp('act', uext[:, ch, :, 15:31], v3(ps[:, 0:Tn]), R=[pt], W=[UX])
            P.tt('pool', sw[:, 0, :, 1:31], uext[:, ch, :, 1:31], uext[:, ch, :, 0:30], ALU.add, R=[UX], W=[SW])
            P.tt('pool', sw[:, 1, :, 3:31], sw[:, 0, :, 3:31], sw[:, 0, :, 1:29], ALU.add, R=[SW], W=[SW])
            if ch == 1:
                P.tt('pool', sw[:, 2, :, 7:31], sw[:, 1, :, 7:31], sw[:, 1, :, 3:27], ALU.add, R=[SW], W=[SW])
                P.tt('pool', sw[:, 3, :, 15:31], sw[:, 2, :, 15:31], sw[:, 2, :, 7:23], ALU.add, R=[SW], W=[SW])
            for hf in range(2):
                g = ch * 2 + hf
                pr = slice(64 * hf, 64 * hf + 64)
                P.stt('dve', v3(dTt[:, ch, :])[pr], sw[pr, g, :, 15:31], 1.0 / POOLW[g], uext[pr, ch, :, 15:31],
                      ALU.mult, ALU.subtract, R=[SW, UX], W=[DT])
            ps, pt = bank([2, 3, 4, 5])
            P.mm(ps[:, 0:Tn], poolW[:, l, ch, :], dTt[:, ch, :], R=[DT, CT], W=[pt])
            P.act(odst[:, ch, :], ps[:, 0:Tn], AF.Identity, R=[pt, CT], W=[ODS], scale=pscol[:, l, ch:ch + 1])
            for i in range(NS):
                P.dma('pool', O['s_pool'][l, i, :, ch * 128:(ch + 1) * 128].rearrange("t c -> c t"), uext[:, ch, i, 16:31], R=[UX])
        P.dma('sp', SS['obr'][3], odst, R=[ODS], WA=[scr_t['obr']])
        P.barrier()
        A.pop()

    def pass_b_s(l):
        A.push()
        NK = KT + 1
        kTs = A.get([5, PL + 128], BF16)
        KTS = Trk()
        vvs = A.get([NK, 4, 192], BF16)
        VVS = Trk()
        P.memset('pool', vvs.rearrange("p k a (s e) -> p (k a) s e", s=3)[:, :, 1, :], 1.0, W=[VVS])
        nFs = A.get([NS, NK, 4], F32)
        NFS = Trk()
        P.dma('sp', nFs, SS['nF'], W=[NFS])
        cqs = A.get([128], BF16)
        CQS = Trk()
        P.dma('sp', cqs[0:4, :], SS['cq'], W=[CQS])
        qTs = A.get([5, 128], BF16)
        QTS = Trk()
        P.dma('sp', qTs, SS['qT'], W=[QTS])
        GK = 4
        stg = A.get([3, GK, 256], F32)
        STG = [Trk() for _ in range(3)]
        pT = A.get([2, KT * 16], BF16)
        PTt = [Trk(), Trk()]
        pTn = A.get([2, 16], BF16)
        PTN = [Trk(), Trk()]
        rb = A.get([2, 16], F32)
        RB = [Trk(), Trk()]
        o1 = A.get([16], F32)
        t2 = A.get([16], F32)
        OT = Trk()
        odf = A.get([2, 16], F32)
        ODF = [Trk(), Trk()]
        sq = A.get([16], F32)
        SQ = Trk()
        obst = A.get([2, 2, 128], BF16)
        OBS = Trk()
        P.memset('dve', obst, 0.0, W=[OBS])
        qm = A.get([12, 128], BF16)
        QM = Trk()
        P.memset('dve', qm, 0.0, W=[QM])
        for hc_ in range(8):
            cj_, r_ = hc_ // 3, hc_ % 3
            P.cp('dve', qm[32 * r_:32 * r_ + 32, hc_, :], qTs[32 * r_:32 * r_ + 32, cj_, :], R=[QTS], WA=[QM])
        for h_ in range(4):
            hh_ = h_ % 2
            P.cp('dve', qm[64 * hh_:64 * hh_ + 64, 8 + h_, :], qTs[64 * hh_:64 * hh_ + 64, 3 + h_ // 2, :], R=[QTS], WA=[QM])
        cnt = {'g': 0, 's': 0, 'a': 0, 'p': 0}
        bstop = cfg.get('bstop', 99)
        for i in range(NS):
          try:
            P.dma('sp', kTs[:, :, PL:PL + 128], SS['kT'], W=[KTS])
            P.dma('sp', vvs[:, KT, :, :], SS['v'], WA=[VVS])
            for br, (kn, vn) in enumerate((('cdk', 'cdv'), ('cfk', 'cfv'))):
                chunks = DCH if br == 0 else [(0, 128), (128, 128)]
                for g in range(KT // GK):
                    si = cnt['g'] % 3
                    cnt['g'] += 1
                    P.dma('sp', stg[:, si], I[kn][l, i, g * GK * 128:(g + 1) * GK * 128, :].rearrange("(k p) f -> p k f", p=128), W=[STG[si]])
                    for j, (o, m) in enumerate(chunks):
                        ps, pt = bank([0, 1, 2])
                        for k in range(GK):
                            P.tr(ps[0:m, k * 128:(k + 1) * 128], stg[:, si, k, o:o + m], identf, R=[STG[si], CT], W=[pt], tick=(k == GK - 1))
                        cj = j if br == 0 else 3 + j
                        P.cp('act' if j % 2 else 'dve', kTs[0:m, cj, g * GK * 128:(g + 1) * GK * 128], ps[0:m, 0:GK * 128], R=[pt], WA=[KTS])
                    si = cnt['g'] % 3
                    cnt['g'] += 1
                    P.dma('sp', stg[:, si], I[vn][l, i, g * GK * 128:(g + 1) * GK * 128, :].rearrange("(k p) f -> p k f", p=128), W=[STG[si]])
                    for k in range(GK):
                        P.cp('pool', vvs[:, g * GK + k, 2 * br:2 * br + 2, :].rearrange("p a (s e) -> p a s e", s=3)[:, :, 0:3:2, :],
                             stg[:, si, k].rearrange("p (a s e) -> p a s e", a=2, s=2), R=[STG[si]], WA=[VVS])
            qs = slice(32 * i, 32 * i + 16)
            if bstop == 1:
                raise _Stop()
            for br in range(2):
                if bstop == 2 and br == 1:
                    raise _Stop()
                for h in range(4):
                    pair, hh = h // 2, h % 2
                    orng = slice(64 * hh, 64 * hh + 64)
                    drng = slice(64 * (1 - hh), 64 * (1 - hh) + 64)
                    accs = []
                    for comp in range(2 if br == 0 else 1):
                        ai = 5 + cnt['a'] % 2
                        cnt['a'] += 1
                        accs.append(ai)
                        acc, acct = PS[ai], PST[ai]
                        ps, pt = bank([3, 4])
                        pi = cnt['p'] % 2
                        cnt['p'] += 1
                        if br == 0:
                            hc = 2 * h + comp
                            cj = hc // 3
                            krows = slice(0, DCH[cj][1])
                            qsel = hc
                        else:
                            cj = 3 + h // 2
                            krows = slice(0, 128)
                            qsel = 8 + h
                        for kt in range(KT):
                            P.mm(ps[:, kt * 16:(kt + 1) * 16], kTs[krows, cj, kt * 128:(kt + 1) * 128], qm[krows, qsel, qs],
                                 start=(kt == 0), stop=False, R=[KTS, QM], W=[pt], tick=False)
                        if br == 0:
                            P.mm(ps[:, 0:KT * 16], C['onesrow'][0:1, :], C['ktb'][0:1, h].rearrange("p k q -> p (k q)"),
                                 start=False, stop=True, R=[CT], W=[pt])
                            di0 = (0 - dmin) // 128
                            P.act(pT[:, pi, :], ps[:, 0:KT * 16], AF.Exp, R=[pt, CT], W=[PTt[pi]], bias=C['alibi'][:, h, di0:di0 + 1], scale=1.0)
                        else:
                            P.mm(ps[:, 0:KT * 16].rearrange("p (k q) -> p k q", q=16), C['sel4'][0:4, h, :],
                                 cqs[0:4, qs].unsqueeze(1).to_broadcast([4, KT, 16]), start=False, stop=True, R=[CT, CQS], W=[pt])
                            for kt in range(KT):
                                P.act(pT[:, pi, kt * 16:(kt + 1) * 16], ps[:, kt * 16:(kt + 1) * 16], AF.Exp, R=[pt, NFS], W=[PTt[pi]],
                                      bias=nFs[:, i, kt, h:h + 1], scale=1.0)
                        if bstop == 3:
                            raise _Stop()
                        psn, ptn = PS[7], PST[7]
                        P.mm(psn[:, 0:16], kTs[krows, cj, PL:PL + 128], qm[krows, qsel, qs], start=True, stop=False, R=[KTS, QM], W=[ptn], tick=False)
                        if br == 0:
                            P.mm(psn[:, 0:16], identb, C['mnd'][:, h, i, :], start=False, stop=True, R=[CT], W=[ptn])
                            P.act(pTn[:, pi, :], psn[:, 0:16], AF.Exp, R=[ptn], W=[PTN[pi]])
                        else:
                            P.mm(psn[:, 0:16], C['sel4'][0:4, h, :], cqs[0:4, qs], start=False, stop=False, R=[CT, CQS], W=[ptn], tick=False)
                            P.mm(psn[:, 0:16], identb, C['mnf'][:, i, :], start=False, stop=True, R=[CT], W=[ptn])
                            P.act(pTn[:, pi, :], psn[:, 0:16], AF.Exp, R=[ptn, NFS], W=[PTN[pi]], bias=nFs[:, i, KT, h:h + 1], scale=1.0)
                        if bstop == 4:
                            raise _Stop()
                        for kt in range(KT):
                            P.mm(acc[:, 0:16], vvs[:, kt, br * 2 + pair, 64 * hh:64 * hh + 128], pT[:, pi, kt * 16:(kt + 1) * 16],
                                 start=(kt == 0), stop=False, R=[VVS, PTt[pi]], W=[acct], tick=False)
                        P.mm(acc[:, 0:16], vvs[:, KT, br * 2 + pair, 64 * hh:64 * hh + 128], pTn[:, pi, :],
                             start=False, stop=True, R=[VVS, PTN[pi]], W=[acct])
                        if bstop == 6:
                            raise _Stop()
                    if bstop == 5:
                        raise _Stop()
                    if br == 0:
                        a0, a1 = accs
                        P.recip(rb[orng, 0, :], PS[a0][drng, 0:16], R=[PST[a0]], W=[RB[0]])
                        P.tt('dve', o1[orng, :], PS[a0][orng, 0:16], rb[orng, 0, :], ALU.mult, R=[PST[a0], RB[0]], W=[OT])
                        P.recip(rb[orng, 1, :], PS[a1][drng, 0:16], R=[PST[a1]], W=[RB[1]])
                        P.tt('dve', t2[orng, :], PS[a1][orng, 0:16], rb[orng, 1, :], ALU.mult, R=[PST[a1], RB[1], OT], W=[OT])
                        P.stt('dve', odf[orng, pair, :], t2[orng, :], neglam[orng, l:l + 1], o1[orng, :], ALU.mult, ALU.add,
                              R=[OT, CT], W=[ODF[pair]])
                        if hh == 1:
                            P.act(sq, odf[:, pair, :], AF.Square, R=[ODF[pair]], W=[SQ])
                            pm, pmt = PS[7], PST[7]
                            P.mm(pm[:, 16:32], C['blk64'], sq, R=[SQ, CT], W=[pmt])
                            P.act(sq, pm[:, 16:32], AF.Sqrt, R=[pmt, CT], W=[SQ], bias=epsc, scale=1.0)
                            P.recip(sq, sq, R=[SQ], W=[SQ])
                            P.stt('dve', obst[:, 0, pair, qs], odf[:, pair, :], gsub[:, l:l + 1], sq, ALU.mult, ALU.mult,
                                  R=[ODF[pair], SQ, CT], W=[OBS])
                    else:
                        a0 = accs[0]
                        P.recip(rb[orng, 0, :], PS[a0][drng, 0:16], R=[PST[a0]], W=[RB[0]])
                        P.tt('dve', obst[orng, 1, pair, qs], PS[a0][orng, 0:16], rb[orng, 0, :], ALU.mult, R=[PST[a0], RB[0]], W=[OBS])
          except _Stop:
            pass
        for br in range(2):
            P.dma('sp', SS['obr'][1 + br], obst[:, br], R=[OBS], WA=[scr_t['obr']])
        P.barrier()
        A.pop()

    def sample_tiles(l):
        last = (l == NL - 1)
        src = I['xs'] if l == 0 else SS['xres']
        dst = O['ys'] if last else SS['xres']
        return [dict(src=src.rearrange("(b p) d -> p b d", p=128), dst=dst.rearrange("(b p) d -> p b d", p=128),
                     hT=SS['hT'], obr=[SS['obr'][i] for i in range(4)], nb=1)]

    for l in range(NL):
        if l > 0:
            convert_layer(l)
            P.barrier(final=True)
        for s in range(NSEQ):
            if 'A' in cfg.get('passes', 'ABC'):
                pass_a(l, s)
            if 'B' in cfg.get('passes', 'ABC'):
                pass_b(l, s)
            if 'C' in cfg.get('passes', 'ABC'):
                pass_c(l, prompt_tiles(l, s))
        if NS:
            sp = cfg.get('spass', 'ABC')
            if 'A' in sp:
                pass_a_s(l)
            if 'B' in sp:
                pass_b_s(l)
            if 'C' in sp:
                pass_c(l, sample_tiles(l))


_CACHE = {}

WNAMES = ('g_norm', 'w_in', 'b_forget', 'conv_w', 'lambda_qk', 'diff_subln', 'pool_w', 'pool_scale', 'w_branch', 'w_gate',
          'b_gate', 'w_out', 'w_ffn_in', 'w_ffn_out')


def run_all(cfg, inp, ncore=8):
    key = tuple(sorted((k, v) for k, v in cfg.items() if not k.startswith('_')))
    if key not in _CACHE:
        _CACHE[key] = build(cfg)
    nc = _CACHE[key]
    NL, NSEQ, S, NS, PL = cfg['NL'], cfg['NSEQ'], cfg['S'], cfg.get('NS', 0), cfg.get('PL', 0)
    consts = make_consts(S, NS, PL)
    consts.pop('_dmin')
    f32 = lambda a: np.asarray(a, np.float32)
    x_prompt = f32(inp['x_prompt'])
    wts = {w: np.ascontiguousarray(f32(inp[w])) for w in WNAMES}
    in_maps = []
    for c in range(ncore):
        m = {'xp': np.ascontiguousarray(x_prompt[c * NSEQ:(c + 1) * NSEQ])}
        m.update(wts)
        for k, v in consts.items():
            m['c_' + k] = v
        if NS:
            xs = f32(inp['x_sample'])
            DS = xs.shape[1]
            pad = np.zeros((128, D), np.float32)
            for i in range(NS):
                pad[32 * i:32 * i + DS] = xs[c * NS + i]
            m['xs'] = pad
            sl = slice(c * NS, (c + 1) * NS)
            m['cdk'] = np.ascontiguousarray(f32(inp['cache_diff_k'])[:, sl].reshape(NL, NS, PL, 256))
            m['cdv'] = np.ascontiguousarray(f32(inp['cache_diff_v'])[:, sl].reshape(NL, NS, PL, 256))
            m['cfk'] = np.ascontiguousarray(f32(inp['cache_fox_k'])[:, sl].reshape(NL, NS, PL, 256))
            m['cfv'] = np.ascontiguousarray(f32(inp['cache_fox_v'])[:, sl].reshape(NL, NS, PL, 256))
            m['clf'] = np.ascontiguousarray(f32(inp['cache_fox_lf'])[:, sl])
            m['sconv'] = np.ascontiguousarray(f32(inp['state_conv'])[:, sl])
            m['spool'] = np.ascontiguousarray(f32(inp['state_pool'])[:, sl])
        in_maps.append(m)
    res = run_bass_kernel_spmd(nc, in_maps, core_ids=list(range(ncore))).results
    B = NSEQ * ncore
    o = {}
    o['yp'] = np.concatenate([r['yp'] for r in res], axis=0)
    cat1 = lambda nm: np.concatenate([r[nm] for r in res], axis=1)
    o['p_dk'] = cat1('p_dk').reshape(NL, B, S, 4, 2, 32)
    o['p_dv'] = cat1('p_dv').reshape(NL, B, S, 4, 64)
    o['p_fk'] = cat1('p_fk').reshape(NL, B, S, 4, 64)
    o['p_fv'] = cat1('p_fv').reshape(NL, B, S, 4, 64)
    o['p_lf'] = cat1('p_lf')
    o['p_conv'] = cat1('p_conv')
    o['p_pool'] = cat1('p_pool')
    if NS:
        DS = np.asarray(inp['x_sample']).shape[1]
        rows = np.concatenate([np.arange(32 * i, 32 * i + DS) for i in range(NS)])
        o['ys'] = np.concatenate([r['ys'][rows].reshape(NS, DS, D) for r in res], axis=0)

        def tok(nm, shp):
            return np.concatenate([r[nm][:, rows].reshape((NL, NS, DS) + shp) for r in res], axis=1)
        o['s_dk'] = tok('s_dk', (4, 2, 32))
        o['s_dv'] = tok('s_dv', (4, 64))
        o['s_fk'] = tok('s_fk', (4, 64))
        o['s_fv'] = tok('s_fv', (4, 64))
        o['s_lf'] = tok('s_lf', (4,))
        o['s_conv'] = cat1('s_conv')
        o['s_pool'] = cat1('s_pool')
    if cfg.get('debug'):
        for nm in ('obrs_scr',):
            if nm in res[0]:
                o['dbg_' + nm] = np.asarray(res[0][nm]).astype(np.float32)
    return o


ONAMES = ('yp', 'ys', 'p_dk', 'p_dv', 'p_fk', 'p_fv', 'p_lf', 'p_conv', 'p_pool', 's_dk', 's_dv', 's_fk', 's_fv', 's_lf', 's_conv', 's_pool')


def kernel(**inp):
    B, S, _ = inp['x_prompt'].shape
    NL = inp['g_norm'].shape[0]
    DB = inp['x_sample'].shape[0]
    PL = inp['cache_diff_k'].shape[2]
    cfg = dict(NL=NL, NSEQ=B // 8, S=S, NS=DB // 8, PL=PL)
    import os
    if os.environ.get('BSTOP'):
        cfg['bstop'] = int(os.environ['BSTOP'])
    if os.environ.get('SPASS'):
        cfg['spass'] = os.environ['SPASS']
    if os.environ.get('PPASS') is not None:
        cfg['passes'] = os.environ['PPASS']
    o = run_all(cfg, inp)
    return tuple(np.ascontiguousarray(o[k], dtype=np.float32) for k in ONAMES)
```
